# Optimizing a Trainium2 kernel written in Bass

```python
import jax, jax.numpy as jnp
from jax import lax
import numpy as np

D_MODEL = 4096
BATCH = 8
SEQ = 2048
DEPTH = 4
DEC_BATCH = 4
DEC_SEQ = 4096
PAST_LEN = 128

N_EVEN = (DEPTH + 1) // 2
N_ODD = DEPTH // 2
D_FF = 3 * D_MODEL // 8
EPS = 1e-6
D_MIX = D_MODEL // 2
MIX_HALF = D_MIX // 2

A_HEAD_DIM = 128
A_HEADS = MIX_HALF // A_HEAD_DIM
A_KV_HEADS = 2
A_WINDOW = 128
A_BLOCK = 128
A_Q = A_HEADS * A_HEAD_DIM
A_KV = A_KV_HEADS * A_HEAD_DIM

B_CH = MIX_HALF
B_CONV = 31
EVEN_IN = A_Q + 2 * A_KV + 2 * B_CH
EVEN_OUT = A_Q + B_CH

C_HEADS = 4
C_DV = MIX_HALF // C_HEADS
C_DK = C_DV // 2
C_RANK = 16
C_TAU = 16.0
C_CHUNK = 16
C_QK = C_HEADS * C_DK
C_V = C_HEADS * C_DV

D_WIDTH = MIX_HALF
D_BLOCKS = 8
D_BLOCK_DIM = D_WIDTH // D_BLOCKS
D_CONV = 4
D_C = 8.0
ODD_IN = 2 * C_QK + 2 * C_V + 2 * C_RANK + 2 * D_WIDTH
ODD_OUT = C_V + D_WIDTH

kernel_name = "hybrid_bidir_encoder_trunk"


def rmsnorm(x, g):
    xf = x.astype(jnp.float32)
    inv = lax.rsqrt(jnp.mean(xf * xf, axis=-1, keepdims=True) + EPS)
    return (xf * inv).astype(x.dtype) * g


def layernorm(x, g, b):
    xf = x.astype(jnp.float32)
    mu = jnp.mean(xf, axis=-1, keepdims=True)
    xc = xf - mu
    var = jnp.mean(xc * xc, axis=-1, keepdims=True)
    return (xc * lax.rsqrt(var + EPS)).astype(x.dtype) * g + b


def swiglu_ffn(x, w_in, w_out):
    gate, up = jnp.split(x @ w_in, 2, axis=-1)
    return (jax.nn.silu(gate) * up) @ w_out


def depthwise_conv_centred(x, w, b):
    width = w.shape[0]
    left = (width - 1) // 2
    y = lax.conv_general_dilated(
        x, w[:, None, :], window_strides=(1,), padding=[(left, width - 1 - left)],
        dimension_numbers=("NWC", "WIO", "NWC"), feature_group_count=x.shape[-1])
    return y + b


def windowed_attention(q, k, v, sink):
    bsz, seq, _, dh = q.shape
    g = A_HEADS // A_KV_HEADS
    nb = seq // A_BLOCK
    qb = q.reshape(bsz, nb, A_BLOCK, A_KV_HEADS, g, dh)
    pad = ((0, 0), (A_BLOCK, A_BLOCK), (0, 0), (0, 0))
    kp = jnp.pad(k, pad).reshape(bsz, nb + 2, A_BLOCK, A_KV_HEADS, dh)
    vp = jnp.pad(v, pad).reshape(bsz, nb + 2, A_BLOCK, A_KV_HEADS, dh)
    kband = jnp.concatenate([kp[:, :-2], kp[:, 1:-1], kp[:, 2:]], axis=2)
    vband = jnp.concatenate([vp[:, :-2], vp[:, 1:-1], vp[:, 2:]], axis=2)
    scores = jnp.einsum("bnqkgd,bnskd->bnkgqs", qb, kband).astype(jnp.float32) * (dh ** -0.5)
    qpos = jnp.arange(A_BLOCK)
    kpos = jnp.arange(3 * A_BLOCK) - A_BLOCK
    rel = kpos[None, :] - qpos[:, None]
    abs_k = (jnp.arange(nb) * A_BLOCK)[:, None] + kpos[None, :]
    valid = (jnp.abs(rel) <= A_WINDOW)[None] & ((abs_k >= 0) & (abs_k < seq))[:, None, :]
    slopes = jnp.exp2(-8.0 * jnp.arange(1, A_HEADS + 1, dtype=jnp.float32) / A_HEADS)
    alibi = -slopes.reshape(A_KV_HEADS, g, 1, 1) * jnp.abs(rel).astype(jnp.float32)
    scores = jnp.where(valid[None, :, None, None], scores + alibi[None, None], -jnp.inf)
    sink_logit = jnp.broadcast_to(sink.astype(jnp.float32).reshape(A_KV_HEADS, g, 1, 1),
                                  (bsz, nb, A_KV_HEADS, g, A_BLOCK, 1))
    probs = jax.nn.softmax(jnp.concatenate([scores, sink_logit], axis=-1), axis=-1)[..., :-1]
    out = jnp.einsum("bnkgqs,bnskd->bnqkgd", probs.astype(v.dtype), vband)
    return out.reshape(bsz, seq, A_HEADS * dh)


def even_mixer(h, w_in, w_out, q_gain, k_gain, sink, conv_w, conv_b, cn_g, cn_b):
    bsz, seq, _ = h.shape
    proj = h @ w_in
    q, k, v, glu = jnp.split(proj, [A_Q, A_Q + A_KV, A_Q + 2 * A_KV], axis=-1)
    q = rmsnorm(q.reshape(bsz, seq, A_HEADS, A_HEAD_DIM), q_gain)
    k = rmsnorm(k.reshape(bsz, seq, A_KV_HEADS, A_HEAD_DIM), k_gain)
    v = v.reshape(bsz, seq, A_KV_HEADS, A_HEAD_DIM)
    a_out = windowed_attention(q, k, v, sink)
    u, gate = jnp.split(glu, 2, axis=-1)
    c = depthwise_conv_centred(u * jax.nn.sigmoid(gate), conv_w, conv_b)
    c = jax.nn.silu(layernorm(c, cn_g, cn_b))
    return jnp.concatenate([a_out, c], axis=-1) @ w_out


def gla_causal(q, k, v, log_a):
    dt = v.dtype
    q, k, v = q.astype(jnp.float32), k.astype(jnp.float32), v.astype(jnp.float32)
    bsz, seq, nh, dk = q.shape
    dv = v.shape[-1]
    L = C_CHUNK
    nc = seq // L
    q = q.reshape(bsz, nc, L, nh, dk)
    k = k.reshape(bsz, nc, L, nh, dk)
    v = v.reshape(bsz, nc, L, nh, dv)
    b = jnp.cumsum(log_a.astype(jnp.float32).reshape(bsz, nc, L, nh, dk), axis=2)
    q_in = q * jnp.exp(b)
    k_in = k * jnp.exp(-b)
    causal = jnp.tril(jnp.ones((L, L), dtype=bool))
    att = jnp.einsum("bnthd,bnshd->bnhts", q_in, k_in)
    att = jnp.where(causal, att, 0.0)
    o_intra = jnp.einsum("bnhts,bnshv->bnthv", att, v)
    b_last = b[:, :, -1]
    k_out = k * jnp.exp(b_last[:, :, None] - b)

    def step(state, xs):
        q_c, k_c, v_c, dec_c = xs
        o_c = jnp.einsum("bthd,bhdv->bthv", q_c, state)
        state = dec_c[..., None] * state + jnp.einsum("bthd,bthv->bhdv", k_c, v_c)
        return state, o_c

    xs = (jnp.moveaxis(q_in, 1, 0), jnp.moveaxis(k_out, 1, 0), jnp.moveaxis(v, 1, 0),
          jnp.moveaxis(jnp.exp(b_last), 1, 0))
    state0 = jnp.zeros((bsz, nh, dk, dv), jnp.float32)
    _, o_inter = lax.scan(step, state0, xs)
    o = o_intra + jnp.moveaxis(o_inter, 0, 1)
    return o.reshape(bsz, seq, nh, dv).astype(dt)


def rg_lru(x, wa, ba, wx, bx, lam, reverse):
    bsz, seq, _ = x.shape
    xb = x.reshape(bsz, seq, D_BLOCKS, D_BLOCK_DIM)
    r = jax.nn.sigmoid(jnp.einsum("bshi,hij->bshj", xb, wa).reshape(bsz, seq, D_WIDTH) + ba)
    i = jax.nn.sigmoid(jnp.einsum("bshi,hij->bshj", xb, wx).reshape(bsz, seq, D_WIDTH) + bx)
    log_a = D_C * r.astype(jnp.float32) * jax.nn.log_sigmoid(lam.astype(jnp.float32))
    a = jnp.exp(log_a)
    u = jnp.sqrt(-jnp.expm1(2.0 * log_a)) * (i * x).astype(jnp.float32)

    def step(h, au):
        a_t, u_t = au
        h = a_t * h + u_t
        return h, h

    h0 = jnp.zeros((bsz, D_WIDTH), jnp.float32)
    _, hs = lax.scan(step, h0, (jnp.moveaxis(a, 1, 0), jnp.moveaxis(u, 1, 0)), reverse=reverse)
    return jnp.moveaxis(hs, 0, 1).astype(x.dtype)


def odd_mixer(h, w_in, w_out, gate_up, gate_bias, c_norm_g, conv_w, conv_b, wa, ba, wx, bx, lam):
    bsz, seq, _ = h.shape
    proj = h @ w_in
    o1 = C_QK
    o2 = o1 + C_QK
    o3 = o2 + C_V
    o4 = o3 + C_V
    o5 = o4 + 2 * C_RANK
    o6 = o5 + D_WIDTH
    q, k, v, og, lr, xr, yg = jnp.split(proj, [o1, o2, o3, o4, o5, o6], axis=-1)
    q = q.reshape(bsz, seq, C_HEADS, C_DK) * (C_DK ** -0.5)
    k = k.reshape(bsz, seq, C_HEADS, C_DK)
    v = v.reshape(bsz, seq, C_HEADS, C_DV)
    lr_f, lr_b = jnp.split(lr, 2, axis=-1)
    loga_f = (jax.nn.log_sigmoid((lr_f @ gate_up[0] + gate_bias[0]).astype(jnp.float32)) / C_TAU
              ).reshape(bsz, seq, C_HEADS, C_DK)
    loga_b = (jax.nn.log_sigmoid((lr_b @ gate_up[1] + gate_bias[1]).astype(jnp.float32)) / C_TAU
              ).reshape(bsz, seq, C_HEADS, C_DK)
    o_f = gla_causal(q, k, v, loga_f)
    o_b = jnp.flip(gla_causal(jnp.flip(q, 1), jnp.flip(k, 1), jnp.flip(v, 1), jnp.flip(loga_b, 1)), 1)
    c_out = rmsnorm(o_f + o_b, c_norm_g).reshape(bsz, seq, C_V) * jax.nn.silu(og)
    xc = depthwise_conv_centred(xr, conv_w, conv_b)
    h_f = rg_lru(xc, wa[0], ba[0], wx[0], bx[0], lam[0], reverse=False)
    h_b = rg_lru(xc, wa[1], ba[1], wx[1], bx[1], lam[1], reverse=True)
    d_out = (h_f + h_b) * jax.nn.gelu(yg)
    return jnp.concatenate([c_out, d_out], axis=-1) @ w_out


def run_trunk(x, params):
    (ln_ffn1, ffn1_w_in, ffn1_w_out, ln_mix, ln_ffn2, ffn2_w_in, ffn2_w_out,
     ev_w_in, ev_w_out, a_q_gain, a_k_gain, a_sink, b_conv_w, b_conv_b, b_norm_g, b_norm_b,
     od_w_in, od_w_out, c_gate_up, c_gate_bias, c_norm_g, d_conv_w, d_conv_b,
     d_wa, d_ba, d_wx, d_bx, d_lambda) = params
    for layer in range(DEPTH):
        x = x + 0.5 * swiglu_ffn(rmsnorm(x, ln_ffn1[layer]), ffn1_w_in[layer], ffn1_w_out[layer])
        h = rmsnorm(x, ln_mix[layer])
        j = layer // 2
        if layer % 2 == 0:
            x = x + even_mixer(h, ev_w_in[j], ev_w_out[j], a_q_gain[j], a_k_gain[j], a_sink[j],
                               b_conv_w[j], b_conv_b[j], b_norm_g[j], b_norm_b[j])
        else:
            x = x + odd_mixer(h, od_w_in[j], od_w_out[j], c_gate_up[j], c_gate_bias[j], c_norm_g[j],
                              d_conv_w[j], d_conv_b[j], d_wa[j], d_ba[j], d_wx[j], d_bx[j], d_lambda[j])
        x = x + 0.5 * swiglu_ffn(rmsnorm(x, ln_ffn2[layer]), ffn2_w_in[layer], ffn2_w_out[layer])
    return x


def setup_inputs(seed: int = 0) -> dict:
    key = jax.random.key(seed)
    ks = jax.random.split(key, 32)
    f32 = jnp.float32

    def normal(k, shape, scale):
        return jax.random.normal(k, shape, f32) * scale

    def gain(k, shape):
        return 1.0 + 0.02 * jax.random.normal(k, shape, f32)

    u = jax.random.uniform(ks[30], (N_ODD, 2, D_WIDTH), f32, minval=0.9, maxval=0.999)
    p = u ** (1.0 / D_C)
    d_lambda = jnp.log(p) - jnp.log1p(-p)
    return {
        "x_prompt": normal(ks[0], (BATCH, SEQ, D_MODEL), 1.0),
        "x_sample": normal(ks[1], (DEC_BATCH, DEC_SEQ, D_MODEL), 1.0),
        "ln_ffn1": gain(ks[2], (DEPTH, D_MODEL)),
        "ffn1_w_in": normal(ks[3], (DEPTH, D_MODEL, 2 * D_FF), D_MODEL ** -0.5),
        "ffn1_w_out": normal(ks[4], (DEPTH, D_FF, D_MODEL), D_FF ** -0.5),
        "ln_mix": gain(ks[5], (DEPTH, D_MODEL)),
        "ln_ffn2": gain(ks[6], (DEPTH, D_MODEL)),
        "ffn2_w_in": normal(ks[7], (DEPTH, D_MODEL, 2 * D_FF), D_MODEL ** -0.5),
        "ffn2_w_out": normal(ks[8], (DEPTH, D_FF, D_MODEL), D_FF ** -0.5),
        "ev_w_in": normal(ks[9], (N_EVEN, D_MODEL, EVEN_IN), D_MODEL ** -0.5),
        "ev_w_out": normal(ks[10], (N_EVEN, EVEN_OUT, D_MODEL), EVEN_OUT ** -0.5),
        "a_q_gain": gain(ks[11], (N_EVEN, A_HEAD_DIM)),
        "a_k_gain": gain(ks[12], (N_EVEN, A_HEAD_DIM)),
        "a_sink": normal(ks[13], (N_EVEN, A_HEADS), 0.5),
        "b_conv_w": normal(ks[14], (N_EVEN, B_CONV, B_CH), B_CONV ** -0.5),
        "b_conv_b": normal(ks[15], (N_EVEN, B_CH), 0.02),
        "b_norm_g": gain(ks[16], (N_EVEN, B_CH)),
        "b_norm_b": normal(ks[17], (N_EVEN, B_CH), 0.02),
        "od_w_in": normal(ks[18], (N_ODD, D_MODEL, ODD_IN), D_MODEL ** -0.5),
        "od_w_out": normal(ks[19], (N_ODD, ODD_OUT, D_MODEL), ODD_OUT ** -0.5),
        "c_gate_up": normal(ks[20], (N_ODD, 2, C_RANK, C_QK), C_RANK ** -0.5),
        "c_gate_bias": normal(ks[21], (N_ODD, 2, C_QK), 0.1),
        "c_norm_g": gain(ks[22], (N_ODD, C_DV)),
        "d_conv_w": normal(ks[23], (N_ODD, D_CONV, D_WIDTH), D_CONV ** -0.5),
        "d_conv_b": normal(ks[24], (N_ODD, D_WIDTH), 0.02),
        "d_wa": normal(ks[25], (N_ODD, 2, D_BLOCKS, D_BLOCK_DIM, D_BLOCK_DIM), D_BLOCK_DIM ** -0.5),
        "d_ba": normal(ks[26], (N_ODD, 2, D_WIDTH), 0.1),
        "d_wx": normal(ks[27], (N_ODD, 2, D_BLOCKS, D_BLOCK_DIM, D_BLOCK_DIM), D_BLOCK_DIM ** -0.5),
        "d_bx": normal(ks[28], (N_ODD, 2, D_WIDTH), 0.1),
        "d_lambda": d_lambda,
    }


def reference(x_prompt, x_sample, ln_ffn1, ffn1_w_in, ffn1_w_out, ln_mix, ln_ffn2, ffn2_w_in, ffn2_w_out,
              ev_w_in, ev_w_out, a_q_gain, a_k_gain, a_sink, b_conv_w, b_conv_b, b_norm_g, b_norm_b,
              od_w_in, od_w_out, c_gate_up, c_gate_bias, c_norm_g, d_conv_w, d_conv_b,
              d_wa, d_ba, d_wx, d_bx, d_lambda):
    params = (ln_ffn1, ffn1_w_in, ffn1_w_out, ln_mix, ln_ffn2, ffn2_w_in, ffn2_w_out,
              ev_w_in, ev_w_out, a_q_gain, a_k_gain, a_sink, b_conv_w, b_conv_b, b_norm_g, b_norm_b,
              od_w_in, od_w_out, c_gate_up, c_gate_bias, c_norm_g, d_conv_w, d_conv_b,
              d_wa, d_ba, d_wx, d_bx, d_lambda)
    y_prompt = run_trunk(x_prompt, params)
    y_sample = run_trunk(x_sample, params)
    return (y_prompt, y_sample)
```

```python
import numpy as np
from contextlib import ExitStack
import concourse.bass as bass
import concourse.mybir as mybir
from concourse.bass_utils import run_bass_kernel_spmd

F32 = mybir.dt.float32
BF16 = mybir.dt.bfloat16
AF = mybir.ActivationFunctionType
ALU = mybir.AluOpType

D_MODEL = 4096
KCD = D_MODEL // 128
D_FF = 1536
EPS = 1e-6
EVEN_IN = 3584
ODD_IN = 5152
TT = 512
ENGS = ("pe", "act", "dve", "pool", "sp")
BLOCK_ATTR = {"pe": "tensor", "act": "scalar", "dve": "vector", "pool": "gpsimd", "sp": "sync"}
SAME_ENG_WAIT = True


class Res:
    __slots__ = ("name", "w", "r")

    def __init__(self, name):
        self.name = name
        self.w = None
        self.r = []


class DSem:
    __slots__ = ("h", "count", "bar", "sw")

    def __init__(self, h):
        self.h = h
        self.count = 0
        self.bar = 0
        self.sw = False


class Op:
    __slots__ = ("eng", "fn", "deps", "needed", "sigval", "pairs", "dsem", "epoch")

    def __init__(self, eng, fn, epoch):
        self.eng = eng
        self.fn = fn
        self.deps = []
        self.needed = False
        self.sigval = None
        self.pairs = None
        self.dsem = None
        self.epoch = epoch


class Sched:
    def __init__(self, nc, es):
        self.nc = nc
        self.es = es
        self.eng = {"pe": nc.tensor, "act": nc.scalar, "dve": nc.vector, "pool": nc.gpsimd, "sp": nc.sync}
        self.prog = {e: es.enter_context(nc.semaphore("prog_" + e)) for e in ("pe", "act", "dve", "pool")}
        self.sigcount = {e: 0 for e in self.prog}
        self.waited = {e: {} for e in ENGS}
        self.ops = {e: [] for e in ENGS}
        self.lastc = {e: None for e in ENGS}
        self.dsems = []
        self.free_dsems = []
        self.free_sw = []
        self.epoch = 0
        self.nsem = 0

    def dsem(self, sw=False):
        fl = self.free_sw if sw else self.free_dsems
        if fl:
            return fl.pop()
        h = self.es.enter_context(self.nc.semaphore("dsem%d" % self.nsem))
        self.nsem += 1
        d = DSem(h)
        d.sw = sw
        self.dsems.append(d)
        return d

    def release(self, *ds):
        for d in ds:
            (self.free_sw if d.sw else self.free_dsems).append(d)

    def _collect(self, eng, reads, writes):
        deps = []
        for r in reads:
            if r.w is not None:
                deps.append(r.w)
        for w in writes:
            if w.w is not None:
                deps.append(w.w)
            deps.extend(w.r)
        out = []
        seen = set()
        for d in deps:
            if isinstance(d, Op):
                if d.epoch < self.epoch:
                    continue
                if d.eng == eng and (eng in ("pe", "sp") or not SAME_ENG_WAIT):
                    continue
                if id(d) in seen:
                    continue
                seen.add(id(d))
                d.needed = True
                out.append(d)
            else:
                ds, val = d
                if val <= ds.bar:
                    continue
                key = (id(ds), val)
                if key in seen:
                    continue
                seen.add(key)
                out.append(d)
        return out

    def op(self, eng, fn, reads=(), writes=()):
        o = Op(eng, fn, self.epoch)
        o.deps = self._collect(eng, reads, writes)
        for r in reads:
            r.r.append(o)
        for w in writes:
            w.w = o
            w.r = []
        self.ops[eng].append(o)
        self.lastc[eng] = o
        return o

    def dma(self, q, pairs, reads, writes, dsem):
        assert dsem.sw == (q == "pool"), "dma semaphore kind mismatch"
        o = Op(q, None, self.epoch)
        o.pairs = pairs
        o.dsem = dsem
        o.deps = self._collect(q, reads, writes)
        dsem.count += 16 * len(pairs)
        tok = (dsem, dsem.count)
        for r in reads:
            r.r.append(tok)
        for w in writes:
            w.w = tok
            w.r = []
        self.ops[q].append(o)
        return o

    def barrier(self):
        toks = [(d, d.count) for d in self.dsems if d.count > d.bar]
        for e in ENGS:
            o = Op(e, None, self.epoch)
            deps = []
            for x in ("pe", "act", "dve", "pool"):
                if x != e and self.lastc[x] is not None and self.lastc[x].epoch == self.epoch:
                    self.lastc[x].needed = True
                    deps.append(self.lastc[x])
            deps.extend(toks)
            o.deps = deps
            self.ops[e].append(o)
        for d in self.dsems:
            d.bar = d.count
        self.epoch += 1

    def flush(self):
        for e in self.prog:
            c = self.sigcount[e]
            for o in self.ops[e]:
                if o.fn is not None and o.needed:
                    c += 1
                    o.sigval = c
            self.sigcount[e] = c
        with self.nc.Block() as block:
            for ename in ENGS:
                ops = self.ops[ename]
                if not ops:
                    continue

                def body(e, ename=ename, ops=ops):
                    waited = self.waited[ename]
                    for o in ops:
                        need = {}
                        for d in o.deps:
                            if isinstance(d, Op):
                                sem = self.prog[d.eng]
                                key = "p" + d.eng
                                val = d.sigval
                            else:
                                sem = d[0].h
                                key = id(d[0])
                                val = d[1]
                            if waited.get(key, 0) >= val:
                                continue
                            if key not in need or need[key][1] < val:
                                need[key] = (sem, val)
                        for key, (sem, val) in need.items():
                            e.wait_ge(sem, val)
                            waited[key] = val
                        if o.fn is not None:
                            ins = o.fn(e)
                            if o.needed:
                                ins.then_inc(self.prog[ename], 1)
                        elif o.pairs is not None:
                            for (oa, ia) in o.pairs:
                                e.dma_start(out=oa, in_=ia).then_inc(o.dsem.h, 16)

                getattr(block, BLOCK_ATTR[ename])(body)
        self.ops = {e: [] for e in ENGS}


_UID = [0]


def uname(name):
    _UID[0] += 1
    return "t%d_%s" % (_UID[0], name)


class Rot:
    def __init__(self, items):
        self.items = items
        self.i = 0

    def next(self):
        it = self.items[self.i % len(self.items)]
        self.i += 1
        return it


def weight_specs(depth):
    sp = []
    for l in range(depth):
        sp.append((("f1i", l), "ffn1_w_in", l, D_MODEL, 2 * D_FF))
        sp.append((("f1o", l), "ffn1_w_out", l, D_FF, D_MODEL))
        sp.append((("f2i", l), "ffn2_w_in", l, D_MODEL, 2 * D_FF))
        sp.append((("f2o", l), "ffn2_w_out", l, D_FF, D_MODEL))
        if l % 2 == 0:
            sp.append((("mi", l), "ev_w_in", l // 2, D_MODEL, EVEN_IN))
            sp.append((("mo", l), "ev_w_out", l // 2, 2048, D_MODEL))
        else:
            sp.append((("mi", l), "od_w_in", l // 2, D_MODEL, ODD_IN))
            sp.append((("mo", l), "od_w_out", l // 2, 2048, D_MODEL))
    return sp


class Builder:
    def __init__(self, cfg):
        self.cfg = cfg
        self.NTOK = cfg["NTOK"]
        self.SEG = cfg["SEG"]
        self.DEPTH = cfg["DEPTH"]
        self.NT = self.NTOK // TT
        self.stop_after = cfg.get("stop_after")
        self.debug = cfg.get("debug", False)
        self.nc = bass.Bass("TRN2", target_bir_lowering=False)
        self.es = ExitStack()

    def declare(self):
        nc, NTOK, DEPTH = self.nc, self.NTOK, self.DEPTH
        n_even = (DEPTH + 1) // 2
        n_odd = DEPTH // 2
        I = {}

        def inp(name, shape, dt=F32):
            I[name] = nc.dram_tensor(name, list(shape), dt, kind="ExternalInput").ap()

        inp("xT", [D_MODEL, NTOK])
        inp("carry", [128, 1])
        inp("ffn1_w_in", [DEPTH, D_MODEL, 2 * D_FF])
        inp("ffn1_w_out", [DEPTH, D_FF, D_MODEL])
        inp("ffn2_w_in", [DEPTH, D_MODEL, 2 * D_FF])
        inp("ffn2_w_out", [DEPTH, D_FF, D_MODEL])
        inp("ev_w_in", [n_even, D_MODEL, EVEN_IN])
        inp("ev_w_out", [n_even, 2048, D_MODEL])
        if n_odd:
            inp("od_w_in", [n_odd, D_MODEL, ODD_IN])
            inp("od_w_out", [n_odd, 2048, D_MODEL])
        inp("lnp", [128, 3, DEPTH, KCD])
        inp("ident", [128, 128])
        inp("abias", [128, 8, 384])
        inp("evp", [128, n_even, 2 + 8 + 8 * 31 + 8 * 3])
        if n_odd:
            inp("odp", [128, n_odd, NOD])
            inp("gup", [n_odd, 2, 16, 512])
            inp("dwa", [n_odd, 2, 8, 128, 128])
            inp("dwx", [n_odd, 2, 8, 128, 128])
            inp("gmask", [128, 2, 128])
        self.I = I
        kind = "ExternalOutput"
        self.yT = nc.dram_tensor("yT", [D_MODEL, NTOK], F32, kind=kind).ap()
        dk = "ExternalOutput" if self.debug else "Internal"
        self.P = nc.dram_tensor("Pscr", [ODD_IN, NTOK], BF16, kind=dk).ap()
        self.M = nc.dram_tensor("Mscr", [2048, NTOK], BF16, kind=dk).ap()
        self.Wb = {}
        for key, name, li, K, N in weight_specs(DEPTH):
            nch = (N + 127) // 128
            self.Wb[key] = (nc.dram_tensor("wb_%s_%d" % key, [nch, 128, K], BF16, kind="Internal").ap(), K, N, nch)

    def build(self):
        nc, es = self.nc, self.es
        self.declare()
        with es:
            self.S = Sched(nc, es)
            self.consts()
            self.prepass()
            self.run_phases()
        return nc

    def consts(self):
        nc, es, S, I = self.nc, self.es, self.S, self.I
        DEPTH = self.DEPTH

        def sb(name, shape, dt):
            return es.enter_context(nc.sbuf_tensor(uname(name), list(shape), dt))

        self.lnp = sb("lnp", [128, 3, DEPTH, KCD], F32)
        self.carry = sb("carry", [128, 1], F32)
        self.ones_bf = sb("ones_bf", [128, 128], BF16)
        self.ones_f = sb("ones_f", [128, 128], F32)
        self.ident_f = sb("ident_f", [128, 128], F32)
        self.ident_bf = sb("ident_bf", [128, 128], BF16)
        d = S.dsem()
        r = Res("consts")
        S.dma("sp", [(self.lnp[:], I["lnp"][:, :, :, :]), (self.carry[:], I["carry"][:, :]),
                     (self.ident_f[:], I["ident"][:, :])], [], [r], d)
        S.op("dve", lambda e: e.memset(self.ones_bf[:], 1.0), [], [r])
        S.op("dve", lambda e: e.memset(self.ones_f[:], 1.0), [], [r])
        self.eps_col = sb("eps_col", [128, 1], F32)
        self.one_col = sb("one_col", [128, 1], F32)
        S.op("dve", lambda e: e.memset(self.one_col[:], 1.0), [], [r])
        S.op("dve", lambda e: e.memset(self.eps_col[:], EPS), [], [r])
        S.op("dve", lambda e: e.tensor_copy(out=self.ident_bf[:], in_=self.ident_f[:]), [r], [r])
        S.barrier()
        S.flush()
        S.release(d)

    def prepass(self):
        with ExitStack() as es:
            rel = self.emit_prepass(es, [("f1i", 0), ("f1o", 0), ("mi", 0)], bg=False)
            self.S.barrier()
            self.S.flush()
            self.S.release(*rel)

    def bg_keys(self, k, part):
        if part == 0:
            return [("mo", k), ("f2i", k), ("f2o", k)]
        if k + 1 < self.DEPTH:
            return [("f1i", k + 1), ("f1o", k + 1), ("mi", k + 1)]
        return []

    def emit_prepass(self, es, keys, bg, NB=3):
        nc, S, I = self.nc, self.S, self.I
        if not keys:
            return []
        KG = 4 if bg else 8
        if self.cfg.get("verbose"):
            print("prepass", keys, "sbuf remaining", nc.sbuf_bytes_remaining)
        stf = [es.enter_context(nc.sbuf_tensor(uname("stf"), [128, KG, 512], F32)) for i in range(NB)]
        stb = [es.enter_context(nc.sbuf_tensor(uname("stb"), [128, 4, KG, 128], BF16)) for i in range(NB)]
        rf = [Res("stf%d" % i) for i in range(NB)]
        rb = [Res("stb%d" % i) for i in range(NB)]
        dl = [S.dsem(bg) for _ in range(NB)]
        dst = [S.dsem(bg) for _ in range(NB)]
        specs = {key: (name, li) for key, name, li, K, N in weight_specs(self.DEPTH)}
        items = []
        for key in keys:
            name, li = specs[key]
            wb, K, N, nch = self.Wb[key]
            w = I[name][li]
            kgs = [(k0, min(KG, K // 128 - k0)) for k0 in range(0, K // 128, KG)]
            for n0 in range(0, N, 512):
                ncol = min(512, N - n0)
                for (k0, kn) in kgs:
                    items.append((w, wb, n0, ncol, k0, kn))
        cast_engs = ("pool",) if bg else ("dve", "pool", "act")
        lq = "pool" if bg else "sp"
        sq_ = "pool" if bg else "act"

        def load(i):
            w, wb, n0, ncol, k0, kn = items[i]
            s = i % NB
            src = w[k0 * 128:(k0 + kn) * 128, n0:n0 + ncol].rearrange("(kc p) n -> p kc n", p=128)
            S.dma(lq, [(stf[s][:, 0:kn, 0:ncol], src)], [], [rf[s]], dl[s])

        def cast_store(i):
            w, wb, n0, ncol, k0, kn = items[i]
            s = i % NB
            ce = cast_engs[i % len(cast_engs)]
            if ncol == 512:
                oa = stb[s][:, :, 0:kn, :].rearrange("p nn kc j -> p kc nn j")
                ia = stf[s][:, 0:kn, :].rearrange("p kc (nn j) -> p kc nn j", j=128)
                ncc = 4
            else:
                assert ncol < 128
                oa = stb[s][:, 0, 0:kn, 0:ncol]
                ia = stf[s][:, 0:kn, 0:ncol]
                ncc = 1
            if ce == "act":
                S.op("act", lambda e, oa=oa, ia=ia: e.copy(out=oa, in_=ia), [rf[s]], [rb[s]])
            else:
                S.op(ce, lambda e, oa=oa, ia=ia: e.tensor_copy(out=oa, in_=ia), [rf[s]], [rb[s]])
            c0 = n0 // 128
            dsta = wb[c0:c0 + ncc, :, k0 * 128:(k0 + kn) * 128].rearrange("c p x -> p c x")
            srca = stb[s][:, 0:ncc, 0:kn, :].rearrange("p c kc j -> p c (kc j)")
            S.dma(sq_, [(dsta, srca)], [rb[s]], [], dst[s])

        n = len(items)
        if bg:
            la = NB - 1
            steps = []
            for i in range(n):
                def step(i=i):
                    if i == 0:
                        for j in range(min(la, n)):
                            load(j)
                    if i + la < n:
                        load(i + la)
                    cast_store(i)
                steps.append(step)
            self.bg_steps = steps
        else:
            for i in range(n):
                load(i)
                cast_store(i)
        return dl + dst

    def phase_list(self):
        ph = []
        for k in range(self.DEPTH + 1):
            ph.append(("A", k))
            if k < self.DEPTH:
                ph.append(("B", k))
        if self.stop_after is not None:
            idx = ph.index(tuple(self.stop_after))
            ph = ph[:idx + 1]
        return ph

    def run_phases(self):
        for kind, k in self.phase_list():
            if kind == "A":
                self.phase_A(k)
            elif k % 2 == 0:
                self.phase_B_even(k)
            else:
                self.phase_B_odd(k)

    def phase_A(self, k):
        nc, S, I = self.nc, self.S, self.I
        DEPTH = self.DEPTH
        do_pre = k > 0
        do_post = k < DEPTH
        with ExitStack() as es:
            def sb(name, shape, dt):
                return es.enter_context(nc.sbuf_tensor(uname(name), list(shape), dt))

            X = sb("X", [128, KCD, TT], F32)
            XG = sb("XG", [128, KCD, TT], BF16)
            HID = sb("HID", [128, 12, TT], BF16)
            MT = sb("MT", [128, 16, TT], BF16) if do_pre else None
            NW = 4
            wsl = [sb("wsl%d" % i, [128, KCD, 128], BF16) for i in range(NW)]
            wrot = Rot([(wsl[i], Res("wsl%d" % i), S.dsem()) for i in range(NW)])
            sqrot = Rot([(sb("sq", [128, TT], BF16), Res("sq")) for i in range(5)])
            sgrot = Rot([(sb("sg", [128, TT], F32), Res("sg")) for i in range(2)])
            turot = Rot([(sb("tu", [128, TT], F32), Res("tu")) for i in range(2)])
            stgrot = Rot([(sb("stg", [128, TT], BF16), Res("stg"), S.dsem()) for i in range(4)])
            rinv = sb("rinv", [128, TT], F32)
            r_rinv = Res("rinv")
            ps = [es.enter_context(nc.psum_tensor(uname("psA"), [128, TT], F32)) for i in range(7)]
            psrot = Rot([(ps[i], Res("psA%d" % i)) for i in range(6)])
            psn, r_psn = ps[6], Res("psn")
            rX = [Res("X%d" % i) for i in range(KCD)]
            rXG = [Res("XG%d" % i) for i in range(KCD)]
            rHID = [Res("HID%d" % i) for i in range(12)]
            rMT = Res("MT")
            d_xld, d_xst, d_mld = S.dsem(True), S.dsem(True), S.dsem(True)
            xsrc = I["xT"] if k == 0 else self.yT

            pend = []

            def norm_chunk(kc, lnidx, layer, lag=3):
                sqa, sqr = sqrot.next()
                S.op("act", lambda e, sqa=sqa, kc=kc: e.activation(out=sqa[:], in_=X[:, kc, :], func=AF.Square),
                     [rX[kc]], [sqr])
                pend.append((sqa, sqr, kc))
                S.op("dve", lambda e, kc=kc: e.tensor_scalar(
                    out=XG[:, kc, :], in0=X[:, kc, :], scalar1=self.lnp[:, lnidx, layer, kc:kc + 1], scalar2=None,
                    op0=ALU.mult), [rX[kc]], [rXG[kc]])
                while len(pend) > lag:
                    norm_pe()

            def norm_pe():
                sqa, sqr, kc = pend.pop(0)
                S.op("pe", lambda e, sqa=sqa, kc=kc: e.matmul(psn[:], lhsT=self.ones_bf[:], rhs=sqa[:],
                                                              start=(kc == 0), stop=(kc == KCD - 1)), [sqr], [r_psn])

            def norm_finish():
                while pend:
                    norm_pe()
                S.op("act", lambda e: e.activation(out=rinv[:], in_=psn[:], func=AF.Sqrt, scale=1.0 / D_MODEL,
                                                   bias=self.eps_col[:]), [r_psn], [r_rinv])
                S.op("dve", lambda e: e.reciprocal(out=rinv[:], in_=rinv[:]), [r_rinv], [r_rinv])

            def mm_chunk(wkey, ci, act, ract, KC):
                wb, K, N, nch = self.Wb[wkey]
                M = min(128, N - ci * 128)
                wa, wr, wd = wrot.next()
                S.dma("sp", [(wa[:, 0:KC, :], wb[ci].rearrange("p (kc j) -> p kc j", j=128))], [], [wr], wd)
                pa, pr = psrot.next()

                def fn(e, wa=wa, pa=pa, M=M):
                    ins = None
                    for kc in range(KC):
                        ins = e.matmul(pa[0:M, :], lhsT=wa[:, kc, 0:M], rhs=act[:, kc, :],
                                       start=(kc == 0), stop=(kc == KC - 1))
                    return ins
                S.op("pe", fn, [wr] + list(ract), [pr])
                return pa, pr, M

            def resid(wkey, act, ract, KC, scale, next_norm):
                for ci in range(KCD):
                    pa, pr, M = mm_chunk(wkey, ci, act, ract, KC)
                    S.op("dve", lambda e, pa=pa, ci=ci: e.scalar_tensor_tensor(
                        out=X[:, ci, :], in0=pa[:], scalar=float(scale), in1=X[:, ci, :],
                        op0=ALU.mult, op1=ALU.add), [pr, rX[ci]], [rX[ci]])
                    if next_norm is not None:
                        norm_chunk(ci, *next_norm)
                if next_norm is not None:
                    norm_finish()

            def ffn(which, layer, next_norm):
                wi = ("f%di" % which, layer)
                wo = ("f%do" % which, layer)
                for n in range(12):
                    pg, prg, _ = mm_chunk(wi, n, XG, rXG, KCD)
                    pu, pru, _ = mm_chunk(wi, n + 12, XG, rXG, KCD)
                    sga, sgr = sgrot.next()
                    tua, tur = turot.next()
                    S.op("dve", lambda e, sga=sga, pg=pg: e.tensor_tensor(out=sga[:], in0=pg[:], in1=rinv[:], op=ALU.mult),
                         [prg, r_rinv], [sgr])
                    S.op("act", lambda e, sga=sga: e.activation(out=sga[:], in_=sga[:], func=AF.Silu), [sgr], [sgr])
                    S.op("dve", lambda e, tua=tua, pu=pu: e.tensor_tensor(out=tua[:], in0=pu[:], in1=rinv[:], op=ALU.mult),
                         [pru, r_rinv], [tur])
                    S.op("dve", lambda e, sga=sga, tua=tua, n=n: e.tensor_tensor(
                        out=HID[:, n, :], in0=sga[:], in1=tua[:], op=ALU.mult), [sgr, tur], [rHID[n]])
                resid(wo, HID, rHID, 12, 0.5, next_norm)

            def load_tile(t):
                t0 = t * TT
                if do_pre:
                    pairs = []
                    for g in range(2):
                        pairs.append((MT[:, g * 8:(g + 1) * 8, :],
                                      self.M[g * 1024:(g + 1) * 1024, t0:t0 + TT].rearrange("(kc p) n -> p kc n", p=128)))
                    S.dma("pool", pairs, [], [rMT], d_mld)
                pairs = []
                for g in range(4):
                    pairs.append((X[:, g * 8:(g + 1) * 8, :],
                                  xsrc[g * 1024:(g + 1) * 1024, t0:t0 + TT].rearrange("(kc p) n -> p kc n", p=128)))
                S.dma("pool", pairs, [], rX, d_xld)

            def store_tile(t):
                t0 = t * TT
                pairs = []
                for g in range(4):
                    pairs.append((self.yT[g * 1024:(g + 1) * 1024, t0:t0 + TT].rearrange("(kc p) n -> p kc n", p=128),
                                  X[:, g * 8:(g + 1) * 8, :]))
                S.dma("pool", pairs, rX, [], d_xst)

            self.bg_steps = []
            prel = []
            if k < DEPTH:
                prel = self.emit_prepass(es, self.bg_keys(k, 0) + self.bg_keys(k, 1), bg=True, NB=2)
            nsteps = len(self.bg_steps)
            load_tile(0)
            for t in range(self.NT):
                t0 = t * TT
                for st in self.bg_steps[t * nsteps // self.NT:(t + 1) * nsteps // self.NT]:
                    st()
                if do_pre:
                    resid(("mo", k - 1), MT, [rMT], 16, 1.0, (2, k - 1))
                    ffn(2, k - 1, (0, k) if do_post else None)
                else:
                    for kc in range(KCD):
                        norm_chunk(kc, 0, k)
                    norm_finish()
                if do_post:
                    ffn(1, k, (1, k))
                store_tile(t)
                if t + 1 < self.NT:
                    load_tile(t + 1)
                if do_post:
                    wkey = ("mi", k)
                    nch = self.Wb[wkey][3]
                    for ci in range(nch):
                        pa, pr, M = mm_chunk(wkey, ci, XG, rXG, KCD)
                        sa, sr, sd = stgrot.next()
                        S.op("dve", lambda e, sa=sa, pa=pa, M=M: e.tensor_tensor(out=sa[0:M, :], in0=pa[0:M, :], in1=rinv[0:M, :],
                                                                                op=ALU.mult), [pr, r_rinv], [sr])
                        S.dma("act", [(self.P[ci * 128:ci * 128 + M, t0:t0 + TT], sa[0:M, :])], [sr], [], sd)
            S.barrier()
            S.flush()
            S.release(d_xld, d_xst, d_mld, *prel, *[w[2] for w in wrot.items], *[s[2] for s in stgrot.items])

    def phase_B_even(self, k):
        self.even_attention(k)
        self.even_conv(k)

    def even_attention(self, k):
        nc, S, I = self.nc, self.S, self.I
        l = k // 2
        NTOK, SEG = self.NTOK, self.SEG
        NB = NTOK // 128
        BPS = SEG // 128
        NT = NTOK // TT
        P, Mo = self.P, self.M
        with ExitStack() as es:
            def sb(name, shape, dt):
                return es.enter_context(nc.sbuf_tensor(uname(name), list(shape), dt))

            def pst(name, shape, dt):
                return es.enter_context(nc.psum_tensor(uname(name), list(shape), dt))

            evp = sb("evp", [128, NEV], F32)
            abias = sb("abias", [128, 8, 384], F32)
            esink = sb("esink", [128, 8], F32)
            QK = sb("QK", [128, 10, NTOK], BF16)
            VT = sb("VT", [128, 2, NTOK], BF16)
            VTOK = sb("VTOK", [128, NB, 2, 128], BF16)
            r_par = Res("par")
            d0, d1 = S.dsem(), S.dsem()
            prel = []
            S.dma("sp", [(evp[:], I["evp"][:, l, :]), (abias[:], I["abias"][:, :, :])], [], [r_par], d0)
            S.op("act", lambda e: e.activation(out=esink[:], in_=evp[:, 2:10], func=AF.Exp), [r_par], [r_par])
            etab = sb("etab", [128, 8, 384], BF16)
            S.op("act", lambda e: e.activation(out=etab[:], in_=abias[:], func=AF.Exp), [r_par], [r_par])
            rQK = [[Res("QK%d_%d" % (c, t)) for t in range(NT)] for c in range(10)]
            rVT = Res("VT")
            allqk = [r for row in rQK for r in row]
            S.dma("sp", [(QK[:, c, :], P[c * 128:(c + 1) * 128, :]) for c in range(10)]
                  + [(VT[:, c, :], P[1280 + c * 128:1280 + (c + 1) * 128, :]) for c in range(2)],
                  [], allqk + [rVT], d1)
            sqrot = Rot([(sb("sq", [128, TT], BF16), Res("sq")) for _ in range(6)])
            rirot = Rot([(sb("ri", [128, TT], F32), Res("ri")) for _ in range(6)])
            banks = [(pst("bkE", [128, 512], F32), Res("bkE")) for _ in range(6)]
            psnrot = Rot(list(banks))
            for c in range(10):
                gcol = evp[:, 0:1] if c < 8 else evp[:, 1:2]
                for t in range(NT):
                    sl = slice(t * TT, (t + 1) * TT)
                    sqa, sqr = sqrot.next()
                    ria, rir = rirot.next()
                    pna, pnr = psnrot.next()
                    S.op("act", lambda e, sqa=sqa, c=c, sl=sl: e.activation(out=sqa[:], in_=QK[:, c, sl], func=AF.Square),
                         [rQK[c][t]], [sqr])
                    S.op("pe", lambda e, sqa=sqa, pna=pna: e.matmul(pna[:], lhsT=self.ones_bf[:], rhs=sqa[:],
                                                                    start=True, stop=True), [sqr], [pnr])
                    S.op("act", lambda e, ria=ria, pna=pna: e.activation(out=ria[:], in_=pna[:], func=AF.Sqrt,
                                                                         scale=1.0 / 128, bias=self.eps_col[:]),
                         [pnr], [rir])
                    S.op("dve", lambda e, ria=ria: e.reciprocal(out=ria[:], in_=ria[:]), [rir], [rir])
                    S.op("dve", lambda e, ria=ria, c=c, sl=sl, gcol=gcol: e.scalar_tensor_tensor(
                        out=QK[:, c, sl], in0=QK[:, c, sl], scalar=gcol, in1=ria[:], op0=ALU.mult, op1=ALU.mult),
                        [rir, rQK[c][t]], [rQK[c][t]])
            ptrot = Rot([(pst("pt", [128, 1024], BF16)[:, 0:256].rearrange("p (a b) -> p a b", b=128), Res("pt")) for _ in range(2)])
            rVTOK = [Res("VTOK%d" % m) for m in range(NB)]
            for m in range(NB):
                pta, ptr = ptrot.next()

                def fn(e, pta=pta, m=m):
                    ins = None
                    for kv in range(2):
                        ins = e.transpose(pta[:, kv, :], VT[:, kv, m * 128:(m + 1) * 128], self.ident_bf[:])
                    return ins
                S.op("pe", fn, [rVT], [ptr])
                S.op("act", lambda e, pta=pta, m=m: e.copy(out=VTOK[:, m, :, :], in_=pta[:]), [ptr], [rVTOK[m]])
            pssrot = Rot([(banks[i][0][:, 0:384], banks[i][1]) for i in range(3)])
            psorot = Rot([(banks[3 + i][0][:, 0:256].rearrange("p (a b) -> p a b", b=128), banks[3 + i][1]) for i in range(3)])
            sbtrot = Rot([(sb("sbt", [128, 384], BF16), Res("sbt")) for _ in range(4)])
            ptrot2 = Rot([(sb("PT", [128, 384], BF16), Res("PT")) for _ in range(4)])
            denrot = Rot([(sb("den", [128, 128], F32), Res("den")) for _ in range(4)])
            stgrot = Rot([(sb("ao", [128, 8, TT], BF16), Res("ao"), S.dsem()) for _ in range(2)])
            scale = 128.0 ** -0.5
            sta = None
            for n in range(NB):
                if n % 4 == 0:
                    sta, str_, std = stgrot.next()
                js = []
                for j in range(3):
                    m = n - 1 + j
                    if 0 <= m < NB:
                        js.append((j, m, (m // BPS) != (n // BPS)))
                jlo, jhi = js[0][0], js[-1][0]
                cs = slice(jlo * 128, (jhi + 1) * 128)
                for h in range(8):
                    kv = h // 4
                    psa, psr = pssrot.next()
                    poa, por = psorot.next()
                    sba, sbr = sbtrot.next()
                    pta, ptr = ptrot2.next()
                    dna, dnr = denrot.next()

                    def fn(e, psa=psa, js=js, kv=kv, h=h, n=n):
                        ins = None
                        for (j, m, cross) in js:
                            ins = e.matmul(psa[:, j * 128:(j + 1) * 128], lhsT=QK[:, 8 + kv, m * 128:(m + 1) * 128],
                                           rhs=QK[:, h, n * 128:(n + 1) * 128], start=True, stop=True)
                        return ins
                    rk = list({id(r): r for r in [rQK[8 + kv][m // 4] for (_, m, _) in js]}.values())
                    S.op("pe", fn, rk + [rQK[h][n // 4]], [psr])
                    S.op("act", lambda e, sba=sba, psa=psa, cs=cs: e.activation(out=sba[:, cs], in_=psa[:, cs], func=AF.Exp,
                                                                                scale=scale), [psr], [sbr])
                    S.op("dve", lambda e, sba=sba, pta=pta, h=h, cs=cs: e.tensor_tensor(
                        out=pta[:, cs], in0=sba[:, cs], in1=etab[:, h, cs], op=ALU.mult), [sbr, r_par], [ptr])
                    for (j, m, cross) in js:
                        if cross:
                            S.op("dve", lambda e, pta=pta, j=j: e.tensor_scalar(
                                out=pta[:, j * 128:(j + 1) * 128], in0=pta[:, j * 128:(j + 1) * 128],
                                scalar1=self.carry[:, 0:1], scalar2=None, op0=ALU.mult), [ptr], [ptr])

                    def fn2(e, poa=poa, pta=pta, js=js, kv=kv):
                        ins = None
                        for i, (j, m, cross) in enumerate(js):
                            ins = e.matmul(poa[:, 0, :], lhsT=VTOK[:, m, kv, :], rhs=pta[:, j * 128:(j + 1) * 128],
                                           start=(i == 0), stop=(i == len(js) - 1))
                        for i, (j, m, cross) in enumerate(js):
                            ins = e.matmul(poa[:, 1, :], lhsT=self.ones_bf[:], rhs=pta[:, j * 128:(j + 1) * 128],
                                           start=(i == 0), stop=(i == len(js) - 1))
                        return ins
                    S.op("pe", fn2, [ptr] + [rVTOK[m] for (_, m, _) in js], [por])
                    S.op("dve", lambda e, dna=dna, poa=poa, h=h: e.tensor_scalar(
                        out=dna[:], in0=poa[:, 1, :], scalar1=esink[:, h:h + 1], scalar2=None, op0=ALU.add),
                        [por, r_par], [dnr])
                    S.op("dve", lambda e, dna=dna: e.reciprocal(out=dna[:], in_=dna[:]), [dnr], [dnr])
                    q4 = n % 4
                    S.op("dve", lambda e, dna=dna, poa=poa, sta=sta, h=h, q4=q4: e.tensor_tensor(
                        out=sta[:, h, q4 * 128:(q4 + 1) * 128], in0=poa[:, 0, :], in1=dna[:], op=ALU.mult),
                        [por, dnr], [str_])
                if n % 4 == 3:
                    t0 = (n - 3) * 128
                    S.dma("sp", [(Mo[0:1024, t0:t0 + TT].rearrange("(h p) n -> p h n", p=128), sta[:])], [str_], [], std)
            S.barrier()
            S.flush()
            S.release(d0, d1, *prel, *[x[2] for x in stgrot.items])

    def even_conv(self, k):
        nc, S, I = self.nc, self.S, self.I
        l = k // 2
        NTOK, SEG = self.NTOK, self.SEG
        NSEG = NTOK // SEG
        SEGP = SEG + 30
        TT = min(512, SEG)
        P, Mo = self.P, self.M
        with ExitStack() as es:
            def sb(name, shape, dt):
                return es.enter_context(nc.sbuf_tensor(uname(name), list(shape), dt))

            def pst(name, shape, dt):
                return es.enter_context(nc.psum_tensor(uname(name), list(shape), dt))

            evp = sb("evp", [128, NEV], F32)
            r_par = Res("par")
            d0 = S.dsem()
            prel = []
            S.dma("sp", [(evp[:], I["evp"][:, l, :])], [], [r_par], d0)
            CV = sb("CV", [128, 8, SEG], F32)
            rCV = [Res("CV%d" % c) for c in range(8)]
            ugrot = Rot([(sb("U", [128, SEGP], BF16), sb("G", [128, SEGP], BF16), Res("UG"), S.dsem()) for _ in range(2)])
            sigrot = Rot([(sb("sig", [128, SEGP], F32), Res("sig")) for _ in range(2)])
            glurot = Rot([(sb("glu", [128, SEGP], BF16), Res("glu")) for _ in range(2)])
            dgrot = Rot([(sb("DG", [128, 31, 128], BF16), Res("DG")) for _ in range(2)])
            pcrot = Rot([(pst("pc", [128, 512], F32)[:, 0:TT], Res("pc")) for _ in range(3)])
            mu = sb("mu", [128, TT], F32)
            var = sb("var", [128, TT], F32)
            rstd = sb("rstd", [128, TT], F32)
            r_mu, r_var, r_rstd = Res("mu"), Res("var"), Res("rstd")
            sqcrot = Rot([(sb("sqc", [128, TT], F32), Res("sqc")) for _ in range(2)])
            tmprot = Rot([(sb("tmp", [128, TT], F32), Res("tmp")) for _ in range(2)])
            stgrot = Rot([(sb("co", [128, 8, TT], BF16), Res("co"), S.dsem()) for _ in range(2)])
            ps1, ps2 = pst("ps1", [128, 512], F32)[:, 0:TT], pst("ps2", [128, 512], F32)[:, 0:TT]
            r_ps1, r_ps2 = Res("ps1"), Res("ps2")
            for s in range(NSEG):
                g0 = s * SEG - 15
                lo = max(0, g0)
                hi = min(NTOK, (s + 1) * SEG + 15)
                o_lo, o_hi = lo - g0, hi - g0
                for c in range(8):
                    ua, ga, ugr, ugd = ugrot.next()
                    row = 1536 + c * 128
                    S.dma("sp", [(ua[:, o_lo:o_hi], P[row:row + 128, lo:hi]),
                                 (ga[:, o_lo:o_hi], P[row + 1024:row + 1024 + 128, lo:hi])], [], [ugr], ugd)
                    sga, sgr = sigrot.next()
                    gla, glr = glurot.next()
                    ce = "dve"
                    S.op("act", lambda e, sga=sga, ga=ga, o_lo=o_lo, o_hi=o_hi: e.activation(out=sga[:, o_lo:o_hi], in_=ga[:, o_lo:o_hi],
                                                                       func=AF.Sigmoid), [ugr], [sgr])
                    if o_lo > 0:
                        S.op(ce, lambda e, gla=gla, o_lo=o_lo: e.memset(gla[:, 0:o_lo], 0.0), [], [glr])
                    if o_hi < SEGP:
                        S.op(ce, lambda e, gla=gla, o_hi=o_hi: e.memset(gla[:, o_hi:SEGP], 0.0), [], [glr])
                    S.op(ce, lambda e, gla=gla, sga=sga, ua=ua, o_lo=o_lo, o_hi=o_hi: e.tensor_tensor(
                        out=gla[:, o_lo:o_hi], in0=ua[:, o_lo:o_hi], in1=sga[:, o_lo:o_hi], op=ALU.mult),
                        [ugr, sgr], [glr])
                    if s > 0:
                        S.op(ce, lambda e, gla=gla: e.tensor_scalar(out=gla[:, 0:15], in0=gla[:, 0:15],
                                                                     scalar1=self.carry[:, 0:1], scalar2=None,
                                                                     op0=ALU.mult), [glr], [glr])
                    if s < NSEG - 1:
                        S.op(ce, lambda e, gla=gla: e.tensor_scalar(out=gla[:, SEG + 15:SEGP], in0=gla[:, SEG + 15:SEGP],
                                                                     scalar1=self.carry[:, 0:1], scalar2=None,
                                                                     op0=ALU.mult), [glr], [glr])
                    wc = 10 + c * 31
                    dga, dgr = dgrot.next()
                    for j in range(31):
                        S.op("dve", lambda e, dga=dga, j=j, wc=wc: e.tensor_scalar(
                            out=dga[:, j, :], in0=self.ident_bf[:], scalar1=evp[:, wc + j:wc + j + 1], scalar2=None,
                            op0=ALU.mult), [r_par], [dgr])
                    for t in range(SEG // TT):
                        pca, pcr = pcrot.next()

                        def fnc(e, pca=pca, dga=dga, gla=gla, t=t):
                            ins = None
                            for j in range(31):
                                ins = e.matmul(pca, lhsT=dga[:, j, :], rhs=gla[:, t * TT + j:t * TT + j + TT],
                                               start=(j == 0), stop=(j == 30))
                            return ins
                        S.op("pe", fnc, [dgr, glr], [pcr])
                        S.op("act", lambda e, pca=pca, c=c, t=t: e.activation(
                            out=CV[:, c, t * TT:(t + 1) * TT], in_=pca, func=AF.Identity, bias=evp[:, 258 + c:259 + c]),
                            [pcr, r_par], [rCV[c]])
                for t in range(SEG // TT):
                    sl = slice(t * TT, (t + 1) * TT)
                    t0 = s * SEG + t * TT
                    for c in range(8):
                        S.op("pe", lambda e, c=c, sl=sl: e.matmul(ps1[:], lhsT=self.ones_f[:], rhs=CV[:, c, sl],
                                                                  start=(c == 0), stop=(c == 7)), [rCV[c]], [r_ps1])
                    for c in range(8):
                        qa, qr = sqcrot.next()
                        S.op("act", lambda e, qa=qa, c=c, sl=sl: e.activation(out=qa[:], in_=CV[:, c, sl], func=AF.Square),
                             [rCV[c]], [qr])
                        S.op("pe", lambda e, qa=qa, c=c: e.matmul(ps2[:], lhsT=self.ones_f[:], rhs=qa[:],
                                                                  start=(c == 0), stop=(c == 7)), [qr], [r_ps2])
                    S.op("dve", lambda e: e.tensor_scalar(out=mu[:], in0=ps1[:], scalar1=1.0 / 1024, scalar2=None,
                                                          op0=ALU.mult), [r_ps1], [r_mu])
                    S.op("dve", lambda e: e.tensor_tensor(out=var[:], in0=mu[:], in1=mu[:], op=ALU.mult), [r_mu], [r_var])
                    S.op("dve", lambda e: e.scalar_tensor_tensor(out=var[:], in0=ps2[:], scalar=1.0 / 1024, in1=var[:],
                                                                 op0=ALU.mult, op1=ALU.subtract), [r_ps2, r_var], [r_var])
                    S.op("act", lambda e: e.activation(out=rstd[:], in_=var[:], func=AF.Sqrt, bias=self.eps_col[:]),
                         [r_var], [r_rstd])
                    S.op("dve", lambda e: e.reciprocal(out=rstd[:], in_=rstd[:]), [r_rstd], [r_rstd])
                    sta, str_, std = stgrot.next()
                    for c in range(8):
                        ta, tr = tmprot.next()
                        S.op("dve", lambda e, ta=ta, c=c, sl=sl: e.tensor_tensor(out=ta[:], in0=CV[:, c, sl], in1=mu[:],
                                                                                 op=ALU.subtract), [rCV[c], r_mu], [tr])
                        S.op("dve", lambda e, ta=ta: e.tensor_tensor(out=ta[:], in0=ta[:], in1=rstd[:], op=ALU.mult),
                             [tr, r_rstd], [tr])
                        S.op("act", lambda e, ta=ta, sta=sta, c=c: e.activation(
                            out=sta[:, c, :], in_=ta[:], func=AF.Silu, scale=evp[:, 266 + c:267 + c],
                            bias=evp[:, 274 + c:275 + c]), [tr, r_par], [str_])
                    S.dma("sp", [(Mo[1024:2048, t0:t0 + TT].rearrange("(c p) n -> p c n", p=128), sta[:])], [str_], [], std)
            S.barrier()
            S.flush()
            S.release(d0, *prel, *[x[3] for x in ugrot.items], *[x[2] for x in stgrot.items])

    def phase_B_odd(self, k):
        parts = self.cfg.get("odd_parts", ("gla", "lru"))
        if "gla" in parts:
            self.odd_gla(k)
        if "lru" in parts:
            self.odd_lru(k)

    def odd_gla(self, k):
        nc, S, I = self.nc, self.S, self.I
        l = k // 2
        NTOK, SEG = self.NTOK, self.SEG
        NSEG = NTOK // SEG
        NCH = NTOK // 128
        NCS = SEG // 128
        LT = min(512, SEG)
        P, Mo = self.P, self.M
        with ExitStack() as es:
            def sb(name, shape, dt):
                return es.enter_context(nc.sbuf_tensor(uname(name), list(shape), dt))

            def pst(name, shape, dt):
                return es.enter_context(nc.psum_tensor(uname(name), list(shape), dt))

            odp = sb("odp", [128, NOD], F32)
            negb = sb("negb", [128, 8], F32)
            gmask = sb("gmask", [128, 2, 128], F32)
            guf = sb("guf", [16, 2, 512], F32)
            gub = sb("gub", [16, 2, 512], BF16)
            lrf = sb("lrf", [16, NTOK], BF16)
            lrb = sb("lrb", [16, NTOK], BF16)
            RMF = sb("RMF", [128, SEG], F32)
            RMB = sb("RMB", [128, SEG], F32)
            r_par = Res("par")
            r_lr = Res("lr")
            d0 = S.dsem()
            S.dma("sp", [(odp[:], I["odp"][:, l, :]), (gmask[:], I["gmask"][:, :, :]),
                         (guf[:], I["gup"][l].rearrange("d r n -> r d n")),
                         (lrf[:], P[3072:3088, :]), (lrb[:], P[3088:3104, :])], [], [r_par, r_lr], d0)
            S.op("dve", lambda e: e.tensor_scalar(out=negb[:], in0=odp[:, 0:8], scalar1=-1.0, scalar2=None, op0=ALU.mult),
                 [r_par], [r_par])
            S.op("dve", lambda e: e.tensor_copy(out=gub[:], in_=guf[:]), [r_par], [r_par])
            S.op("pool", lambda e: e.memset(RMF[:], 1.0), [], [r_par])
            S.op("pool", lambda e: e.memset(RMF[:].rearrange("p (c j) -> p c j", j=128)[:, :, 0:1], 0.0), [r_par], [r_par])
            S.op("pool", lambda e: e.memset(RMB[:], 1.0), [], [r_par])
            S.op("pool", lambda e: e.memset(RMB[:].rearrange("p (c j) -> p c j", j=128)[:, :, 127:128], 0.0), [r_par], [r_par])
            prel = []
            qT = sb("qT", [128, NTOK], BF16)
            kT = sb("kT", [128, NTOK], BF16)
            vT = sb("vT", [128, 2, NTOK], BF16)
            VTOK = sb("VTOKg", [128, NCH, 256], BF16)
            SP = sb("SP", [128, SEG], F32)
            BC = sb("BC", [128, SEG], F32)
            EX = sb("EX", [128, SEG], F32)
            qin = sb("qin", [128, SEG], BF16)
            kin = sb("kin", [128, SEG], BF16)
            kout = sb("kout", [128, SEG], BF16)
            DEC = sb("DEC", [128, NCS], F32)
            OF = sb("OF", [128, 2, NTOK], F32)
            Sf = sb("Sf", [128, 256], F32)
            Sb = sb("Sbf", [128, 256], BF16)
            r_qkv, r_SP, r_BC, r_EX, r_qin, r_kin, r_kout, r_DEC = (Res(n) for n in
                                                                    ("qkv", "SP", "BC", "EX", "qin", "kin", "kout", "DEC"))
            rVTOK = [Res("VTOK%d" % c) for c in range(NCH)]
            rOF = [Res("OF%d" % c) for c in range(NCH)]
            r_Sf, r_Sb = Res("Sf"), Res("Sb")
            d_qkv = S.dsem()
            bankG = pst("bankG", [128, 512], F32)
            r_bankG = Res("bankG")
            psgrot = Rot([(bankG, r_bankG)])
            psn, r_psn = bankG, r_bankG
            attrot = Rot([(pst("att", [128, 512], F32)[:, 0:128], Res("pa")) for _ in range(2)])
            psorot = Rot([(pst("pso", [128, 512], F32)[:, 0:256], Res("po")) for _ in range(2)])
            pkvrot = Rot([(pst("pkv", [128, 512], F32)[:, 0:256], Res("pkv")) for _ in range(1)])
            pktrot = Rot([(pst("pkt", [128, 1024], BF16)[:, 0:128], Res("pkt")) for _ in range(1)])
            pvtrot = Rot([(pst("pvt", [128, 1024], BF16)[:, 0:256], Res("pvt")) for _ in range(1)])
            attmrot = Rot([(sb("attm", [128, 128], BF16), Res("attm")) for _ in range(2)])
            koTrot = Rot([(sb("koT", [128, 128], BF16), Res("koT")) for _ in range(2)])
            sqrot = Rot([(sb("sqg", [128, LT], BF16), Res("sqg")) for _ in range(2)])
            ri = sb("rig", [128, LT], F32)
            r_ri = Res("ri")
            OG = sb("OG", [128, 2, LT], BF16)
            r_OG = Res("OG")
            d_og = S.dsem()
            sgo = sb("sgo", [128, 2, LT], F32)
            r_sgo = Res("sgo")
            t1rot = Rot([(sb("t1", [128, LT], F32), Res("t1")) for _ in range(2)])
            stgrot = Rot([(sb("cstg", [128, 2, LT], BF16), Res("cstg"), S.dsem()) for _ in range(2)])

            for h in range(4):
                S.dma("sp", [(qT[:], P[h * 128:(h + 1) * 128, :]), (kT[:], P[512 + h * 128:512 + (h + 1) * 128, :]),
                             (vT[:, 0, :], P[1024 + h * 256:1024 + h * 256 + 128, :]),
                             (vT[:, 1, :], P[1024 + h * 256 + 128:1024 + h * 256 + 256, :])], [], [r_qkv], d_qkv)
                for c in range(NCH):
                    pa, pr = pvtrot.next()

                    def fn(e, pa=pa, c=c):
                        ins = None
                        for vc in range(2):
                            ins = e.transpose(pa[:, vc * 128:(vc + 1) * 128], vT[:, vc, c * 128:(c + 1) * 128], self.ident_bf[:])
                        return ins
                    S.op("pe", fn, [r_qkv], [pr])
                    S.op("act", lambda e, pa=pa, c=c: e.copy(out=VTOK[:, c, :], in_=pa), [pr], [rVTOK[c]])
                for d in range(2):
                    lr = lrf if d == 0 else lrb
                    RM = RMF if d == 0 else RMB
                    segs = list(range(NSEG)) if d == 0 else list(range(NSEG - 1, -1, -1))
                    first = True
                    for si, s in enumerate(segs):
                        s0 = s * SEG
                        for t in range(SEG // LT):
                            sl = slice(t * LT, (t + 1) * LT)
                            gsl = slice(s0 + t * LT, s0 + (t + 1) * LT)
                            pga, pgr = psgrot.next()
                            S.op("pe", lambda e, pga=pga, d=d, h=h, lr=lr, gsl=gsl: e.matmul(
                                pga[:, 0:LT], lhsT=gub[:, d, h * 128:(h + 1) * 128], rhs=lr[:, gsl], start=True, stop=True),
                                [r_par, r_lr], [pgr])
                            S.op("act", lambda e, pga=pga, d=d, h=h, sl=sl: e.activation(
                                out=EX[:, sl], in_=pga[:, 0:LT], func=AF.Exp, scale=-1.0, bias=negb[:, d * 4 + h:d * 4 + h + 1]),
                                [pgr, r_par], [r_EX])
                            S.op("act", lambda e, sl=sl: e.activation(out=SP[:, sl], in_=EX[:, sl], func=AF.Ln,
                                                                      bias=self.one_col[:]), [r_EX], [r_SP])
                        if d == 0:
                            S.op("dve", lambda e, RM=RM: e.tensor_tensor_scan(out=BC[:], data0=RM[:], data1=SP[:], initial=0.0,
                                                                              op0=ALU.mult, op1=ALU.add), [r_SP, r_par], [r_BC])
                        else:
                            S.op("dve", lambda e, RM=RM: e.tensor_tensor_scan(out=BC[:, ::-1], data0=RM[:, ::-1], data1=SP[:, ::-1],
                                                                              initial=0.0, op0=ALU.mult, op1=ALU.add),
                                 [r_SP, r_par], [r_BC])
                        S.op("act", lambda e: e.activation(out=EX[:], in_=BC[:], func=AF.Exp, scale=-1.0 / 16), [r_BC], [r_EX])
                        S.op("dve", lambda e, s0=s0: e.scalar_tensor_tensor(out=qin[:], in0=qT[:, s0:s0 + SEG], scalar=128.0 ** -0.5,
                                                                          in1=EX[:], op0=ALU.mult, op1=ALU.mult),
                             [r_qkv, r_EX], [r_qin])
                        col = 127 if d == 0 else 0
                        S.op("dve", lambda e, col=col: e.tensor_copy(
                            out=DEC[:], in_=EX[:].rearrange("p (c j) -> p c j", j=128)[:, :, col]), [r_EX], [r_DEC])
                        S.op("act", lambda e: e.activation(out=SP[:], in_=BC[:], func=AF.Exp, scale=1.0 / 16), [r_BC], [r_SP])
                        S.op("dve", lambda e, s0=s0: e.tensor_tensor(out=kin[:], in0=kT[:, s0:s0 + SEG], in1=SP[:], op=ALU.mult),
                             [r_qkv, r_SP], [r_kin])
                        for cl in range(NCS):
                            S.op("dve", lambda e, cl=cl: e.tensor_scalar(
                                out=kout[:, cl * 128:(cl + 1) * 128], in0=kin[:, cl * 128:(cl + 1) * 128],
                                scalar1=DEC[:, cl:cl + 1], scalar2=None, op0=ALU.mult), [r_kin, r_DEC], [r_kout])
                        if si > 0:
                            S.op("dve", lambda e: e.tensor_scalar(out=Sf[:], in0=Sf[:], scalar1=self.carry[:, 0:1], scalar2=None,
                                                                  op0=ALU.mult), [r_Sf], [r_Sf])
                            S.op("act", lambda e: e.copy(out=Sb[:], in_=Sf[:]), [r_Sf], [r_Sb])
                        cls = list(range(NCS)) if d == 0 else list(range(NCS - 1, -1, -1))
                        for ci, cl in enumerate(cls):
                            c = s * NCS + cl
                            ls = slice(cl * 128, (cl + 1) * 128)
                            last = (si == NSEG - 1 and ci == NCS - 1)
                            aa, ar = attrot.next()
                            S.op("pe", lambda e, aa=aa, ls=ls: e.matmul(aa, lhsT=kin[:, ls], rhs=qin[:, ls], start=True, stop=True),
                                 [r_kin, r_qin], [ar])
                            ma, mr = attmrot.next()
                            S.op("dve", lambda e, ma=ma, aa=aa, d=d: e.tensor_tensor(out=ma[:], in0=aa, in1=gmask[:, d, :],
                                                                                     op=ALU.mult), [ar, r_par], [mr])
                            poa, por = psorot.next()

                            def fno(e, poa=poa, ma=ma, ls=ls, c=c, first=first):
                                ins = None
                                for vc in range(2):
                                    vs = slice(vc * 128, (vc + 1) * 128)
                                    if not first:
                                        e.matmul(poa[:, vs], lhsT=Sb[:, vs], rhs=qin[:, ls], start=True, stop=False)
                                    ins = e.matmul(poa[:, vs], lhsT=VTOK[:, c, vs], rhs=ma[:], start=first, stop=True)
                                return ins
                            S.op("pe", fno, [mr, rVTOK[c], r_qin] + ([] if first else [r_Sb]), [por])
                            gs = slice(c * 128, (c + 1) * 128)
                            pview = poa.rearrange("p (v t) -> p v t", t=128)
                            if d == 0:
                                S.op("act", lambda e, pview=pview, gs=gs: e.copy(out=OF[:, :, gs], in_=pview), [por], [rOF[c]])
                            else:
                                S.op("dve", lambda e, pview=pview, gs=gs: e.tensor_tensor(out=OF[:, :, gs], in0=pview, in1=OF[:, :, gs],
                                                                                          op=ALU.add), [por, rOF[c]], [rOF[c]])
                            if not last:
                                kta, ktr = pktrot.next()
                                S.op("pe", lambda e, kta=kta, ls=ls: e.transpose(kta, kout[:, ls], self.ident_bf[:]), [r_kout], [ktr])
                                koa, kor = koTrot.next()
                                S.op("act", lambda e, koa=koa, kta=kta: e.copy(out=koa[:], in_=kta), [ktr], [kor])
                                pka, pkr = pkvrot.next()
                                S.op("pe", lambda e, pka=pka, koa=koa, c=c: e.matmul(pka, lhsT=koa[:], rhs=VTOK[:, c, :],
                                                                                     start=True, stop=True), [kor, rVTOK[c]], [pkr])
                                if first:
                                    S.op("dve", lambda e, pka=pka: e.tensor_copy(out=Sf[:], in_=pka), [pkr], [r_Sf])
                                else:
                                    S.op("dve", lambda e, pka=pka, cl=cl: e.scalar_tensor_tensor(
                                        out=Sf[:], in0=Sf[:], scalar=DEC[:, cl:cl + 1], in1=pka, op0=ALU.mult, op1=ALU.add),
                                        [pkr, r_Sf, r_DEC], [r_Sf])
                                S.op("act", lambda e: e.copy(out=Sb[:], in_=Sf[:]), [r_Sf], [r_Sb])
                            first = False
                for t in range(NTOK // LT):
                    sl = slice(t * LT, (t + 1) * LT)
                    rof = rOF[t * LT // 128:(t + 1) * LT // 128]
                    S.dma("sp", [(OG[:, vc, :], P[2048 + h * 256 + vc * 128:2048 + h * 256 + (vc + 1) * 128, sl]) for vc in range(2)],
                          [], [r_OG], d_og)
                    for vc in range(2):
                        qa, qr = sqrot.next()
                        S.op("act", lambda e, qa=qa, vc=vc, sl=sl: e.activation(out=qa[:], in_=OF[:, vc, sl], func=AF.Square),
                             rof, [qr])
                        S.op("pe", lambda e, qa=qa, vc=vc: e.matmul(psn[:, 0:LT], lhsT=self.ones_bf[:], rhs=qa[:],
                                                                   start=(vc == 0), stop=(vc == 1)), [qr], [r_psn])
                    S.op("act", lambda e: e.activation(out=ri[:], in_=psn[:, 0:LT], func=AF.Sqrt, scale=1.0 / 256,
                                                       bias=self.eps_col[:]), [r_psn], [r_ri])
                    S.op("dve", lambda e: e.reciprocal(out=ri[:], in_=ri[:]), [r_ri], [r_ri])
                    S.op("act", lambda e: e.activation(out=sgo[:], in_=OG[:], func=AF.Silu), [r_OG], [r_sgo])
                    sta, str_, std = stgrot.next()
                    for vc in range(2):
                        ta, tr = t1rot.next()
                        S.op("dve", lambda e, ta=ta, vc=vc, sl=sl: e.scalar_tensor_tensor(
                            out=ta[:], in0=OF[:, vc, sl], scalar=odp[:, 8 + vc:9 + vc], in1=ri[:], op0=ALU.mult, op1=ALU.mult),
                            rof + [r_ri, r_par], [tr])
                        S.op("dve", lambda e, ta=ta, sta=sta, vc=vc: e.tensor_tensor(out=sta[:, vc, :], in0=ta[:], in1=sgo[:, vc, :],
                                                                                     op=ALU.mult), [tr, r_sgo], [str_])
                    S.dma("sp", [(Mo[h * 256:(h + 1) * 256, sl].rearrange("(v p) n -> p v n", p=128), sta[:])], [str_], [], std)
            S.barrier()
            S.flush()
            S.release(d0, d_qkv, d_og, *prel, *[x[2] for x in stgrot.items])

    def odd_lru(self, k):
        nc, S, I = self.nc, self.S, self.I
        l = k // 2
        NTOK, SEG = self.NTOK, self.SEG
        NSEG = NTOK // SEG
        LT = 512
        NT = NTOK // LT
        P, Mo = self.P, self.M
        with ExitStack() as es:
            def sb(name, shape, dt):
                return es.enter_context(nc.sbuf_tensor(uname(name), list(shape), dt))

            def pst(name, shape, dt):
                return es.enter_context(nc.psum_tensor(uname(name), list(shape), dt))

            odp = sb("odp", [128, NOD], F32)
            clam = sb("clam", [128, 16], F32)
            wab = sb("wab", [128, 2, 8, 128], BF16)
            wxb = sb("wxb", [128, 2, 8, 128], BF16)
            r_par = Res("par")
            d0 = S.dsem()
            d0s = S.dsem(True)
            S.dma("pool", [(wab[:], I["dwa"][l].rearrange("d c i j -> i d c j")),
                           (wxb[:], I["dwx"][l].rearrange("d c i j -> i d c j"))], [], [r_par], d0s)
            prel = []
            S.dma("sp", [(odp[:], I["odp"][:, l, :])], [], [r_par], d0)
            S.op("act", lambda e: e.activation(out=clam[:], in_=odp[:, 82:98], func=AF.Exp, scale=-1.0), [r_par], [r_par])
            S.op("act", lambda e: e.activation(out=clam[:], in_=clam[:], func=AF.Ln, bias=self.one_col[:]), [r_par], [r_par])
            S.op("dve", lambda e: e.tensor_scalar(out=clam[:], in0=clam[:], scalar1=-8.0, scalar2=None, op0=ALU.mult),
                 [r_par], [r_par])
            XS = sb("XS", [128, NSEG, SEG + 3], BF16)
            XC = sb("XC", [128, NTOK], F32)
            XCB = sb("XCB", [128, NTOK], BF16)
            YG = sb("YG", [128, NTOK], BF16)
            A = [sb("A%d" % d, [128, NTOK], F32) for d in range(2)]
            U = [sb("U%d" % d, [128, NTOK], F32) for d in range(2)]
            Hh = [sb("H%d" % d, [128, NTOK], F32) for d in range(2)]
            r_XS, r_XC, r_XCB, r_YG = Res("XS"), Res("XC"), Res("XCB"), Res("YG")
            r_A = [Res("A0"), Res("A1")]
            r_U = [Res("U0"), Res("U1")]
            r_H = [Res("H0"), Res("H1")]
            d_x, d_y = S.dsem(), S.dsem()
            psrot = Rot([(pst("psl", [128, 512], F32), Res("psl")) for _ in range(4)])
            Rrot = Rot([(sb("R", [128, LT], F32), Res("R")) for _ in range(2)])
            Irot = Rot([(sb("Ig", [128, LT], F32), Res("Ig")) for _ in range(2)])
            A2rot = Rot([(sb("A2", [128, LT], F32), Res("A2")) for _ in range(2)])
            y2rot = Rot([(sb("y2", [128, LT], F32), Res("y2")) for _ in range(4)])
            sgrot = Rot([(sb("sgl", [128, LT], F32), Res("sgl")) for _ in range(2)])
            stgrot = Rot([(sb("dstg", [128, LT], BF16), Res("dstg"), S.dsem()) for _ in range(2)])
            GLb = [sb("GL", [128, NTOK], BF16) for _ in range(2)]
            r_GL = [Res("GL0"), Res("GL1")]
            hsrot = Rot([(sb("hs", [128, LT], F32), Res("hs")) for _ in range(2)])

            def head(c):
                row = 3104 + c * 128
                S.dma("sp", [(XS[:, s, 1:1 + SEG], P[row:row + 128, s * SEG:(s + 1) * SEG]) for s in range(NSEG)], [], [r_XS], d_x)
                S.dma("sp", [(YG[:], P[4128 + c * 128:4128 + (c + 1) * 128, :])], [], [r_YG], d_y)
                for s in range(NSEG):
                    if s == 0:
                        S.op("dve", lambda e, s=s: e.memset(XS[:, s, 0:1], 0.0), [], [r_XS])
                    else:
                        S.op("dve", lambda e, s=s: e.tensor_scalar(out=XS[:, s, 0:1], in0=XS[:, s - 1, SEG:SEG + 1],
                                                                  scalar1=self.carry[:, 0:1], scalar2=None, op0=ALU.mult),
                             [r_XS], [r_XS])
                    if s == NSEG - 1:
                        S.op("dve", lambda e, s=s: e.memset(XS[:, s, SEG + 1:SEG + 3], 0.0), [], [r_XS])
                    else:
                        S.op("dve", lambda e, s=s: e.tensor_scalar(out=XS[:, s, SEG + 1:SEG + 3], in0=XS[:, s + 1, 1:3],
                                                                  scalar1=self.carry[:, 0:1], scalar2=None, op0=ALU.mult),
                             [r_XS], [r_XS])
                wc = 10 + c * 4
                for s in range(NSEG):
                    xs = slice(s * SEG, (s + 1) * SEG)
                    S.op("dve", lambda e, s=s, xs=xs: e.tensor_scalar(
                        out=XC[:, xs], in0=XS[:, s, 0:SEG], scalar1=odp[:, wc:wc + 1], scalar2=odp[:, 42 + c:43 + c],
                        op0=ALU.mult, op1=ALU.add), [r_XS, r_par], [r_XC])
                    for j in range(1, 4):
                        S.op("dve", lambda e, s=s, xs=xs, j=j: e.scalar_tensor_tensor(
                            out=XC[:, xs], in0=XS[:, s, j:j + SEG], scalar=odp[:, wc + j:wc + j + 1], in1=XC[:, xs],
                            op0=ALU.mult, op1=ALU.add), [r_XS, r_XC, r_par], [r_XC])
                S.op("act", lambda e: e.copy(out=XCB[:], in_=XC[:]), [r_XC], [r_XCB])
                GL, rg = GLb[c % 2], r_GL[c % 2]
                for t in range(NT):
                    sl = slice(t * LT, (t + 1) * LT)
                    ya, yr = y2rot.next()
                    sa, sr = sgrot.next()
                    S.op("pool", lambda e, ya=ya, sl=sl: e.tensor_tensor(out=ya[:], in0=YG[:, sl], in1=YG[:, sl], op=ALU.mult),
                         [r_YG], [yr])
                    S.op("pool", lambda e, ya=ya: e.tensor_scalar(out=ya[:], in0=ya[:], scalar1=0.044715, scalar2=1.0,
                                                                  op0=ALU.mult, op1=ALU.add), [yr], [yr])
                    S.op("pool", lambda e, ya=ya, sl=sl: e.tensor_tensor(out=ya[:], in0=ya[:], in1=YG[:, sl], op=ALU.mult),
                         [yr, r_YG], [yr])
                    S.op("act", lambda e, ya=ya, sa=sa: e.activation(out=sa[:], in_=ya[:], func=AF.Sigmoid, scale=1.5957691216057308),
                         [yr], [sr])
                    S.op("dve", lambda e, sa=sa, sl=sl, GL=GL: e.tensor_tensor(out=GL[:, sl], in0=sa[:], in1=YG[:, sl], op=ALU.mult),
                         [sr, r_YG], [rg])

            def gates(c):
                for d in range(2):
                    pc = d * 8 + c
                    for t in range(NT):
                        sl = slice(t * LT, (t + 1) * LT)
                        pra, prr = psrot.next()
                        pia, pir = psrot.next()
                        S.op("pe", lambda e, pra=pra, d=d, sl=sl: e.matmul(pra[:], lhsT=wab[:, d, c, :], rhs=XCB[:, sl],
                                                                          start=True, stop=True), [r_XCB, r_par], [prr])
                        S.op("pe", lambda e, pia=pia, d=d, sl=sl: e.matmul(pia[:], lhsT=wxb[:, d, c, :], rhs=XCB[:, sl],
                                                                          start=True, stop=True), [r_XCB, r_par], [pir])
                        Ra, Rr = Rrot.next()
                        Ia, Ir = Irot.next()
                        A2a, A2r = A2rot.next()
                        S.op("act", lambda e, Ra=Ra, pra=pra, pc=pc: e.activation(out=Ra[:], in_=pra[:], func=AF.Sigmoid,
                                                                                bias=odp[:, 50 + pc:51 + pc]), [prr, r_par], [Rr])
                        S.op("act", lambda e, Ia=Ia, pia=pia, pc=pc: e.activation(out=Ia[:], in_=pia[:], func=AF.Sigmoid,
                                                                                bias=odp[:, 66 + pc:67 + pc]), [pir, r_par], [Ir])
                        S.op("act", lambda e, Ra=Ra, d=d, sl=sl, pc=pc: e.activation(out=A[d][:, sl], in_=Ra[:], func=AF.Exp,
                                                                                   scale=clam[:, pc:pc + 1]), [Rr, r_par], [r_A[d]])
                        S.op("dve", lambda e, A2a=A2a, d=d, sl=sl: e.tensor_tensor(out=A2a[:], in0=A[d][:, sl], in1=A[d][:, sl],
                                                                                 op=ALU.mult), [r_A[d]], [A2r])
                        S.op("act", lambda e, A2a=A2a: e.activation(out=A2a[:], in_=A2a[:], func=AF.Sqrt, scale=-1.0,
                                                                    bias=self.one_col[:]), [A2r], [A2r])
                        S.op("dve", lambda e, Ia=Ia, sl=sl: e.tensor_tensor(out=Ia[:], in0=Ia[:], in1=XC[:, sl], op=ALU.mult),
                             [Ir, r_XC], [Ir])
                        S.op("dve", lambda e, Ia=Ia, A2a=A2a, d=d, sl=sl: e.tensor_tensor(out=U[d][:, sl], in0=Ia[:], in1=A2a[:],
                                                                                        op=ALU.mult), [Ir, A2r], [r_U[d]])
                    for s in range(1, NSEG):
                        tb = s * SEG if d == 0 else s * SEG - 1
                        S.op("dve", lambda e, d=d, tb=tb: e.tensor_scalar(out=A[d][:, tb:tb + 1], in0=A[d][:, tb:tb + 1],
                                                                         scalar1=self.carry[:, 0:1], scalar2=None, op0=ALU.mult),
                             [r_A[d]], [r_A[d]])

            def scans(c):
                S.op("dve", lambda e: e.tensor_tensor_scan(out=Hh[0][:], data0=A[0][:], data1=U[0][:], initial=0.0,
                                                           op0=ALU.mult, op1=ALU.add), [r_A[0], r_U[0]], [r_H[0]])
                S.op("dve", lambda e: e.tensor_tensor_scan(out=Hh[1][:, ::-1], data0=A[1][:, ::-1], data1=U[1][:, ::-1],
                                                           initial=0.0, op0=ALU.mult, op1=ALU.add),
                     [r_A[1], r_U[1]], [r_H[1]])

            def tail(c):
                GL, rg = GLb[c % 2], r_GL[c % 2]
                for t in range(NT):
                    sl = slice(t * LT, (t + 1) * LT)
                    ha, hr = hsrot.next()
                    S.op("dve", lambda e, ha=ha, sl=sl: e.tensor_tensor(out=ha[:], in0=Hh[0][:, sl], in1=Hh[1][:, sl], op=ALU.add),
                         [r_H[0], r_H[1]], [hr])
                    sta, str_, std = stgrot.next()
                    S.op("dve", lambda e, ha=ha, sta=sta, sl=sl, GL=GL: e.tensor_tensor(out=sta[:], in0=ha[:], in1=GL[:, sl], op=ALU.mult),
                         [hr, rg], [str_])
                    S.dma("sp", [(Mo[1024 + c * 128:1024 + (c + 1) * 128, sl], sta[:])], [str_], [], std)

            head(0)
            for c in range(8):
                gates(c)
                if c + 1 < 8:
                    head(c + 1)
                scans(c)
                tail(c)
            S.barrier()
            S.flush()
            S.release(d0, d0s, d_x, d_y, *prel, *[x[2] for x in stgrot.items])


NEV = 282
NOD = 128


def host_consts():
    out = {}
    out["ident"] = np.eye(128, dtype=np.float32)
    sp = np.arange(128)[:, None]
    tq = np.arange(128)[None, :]
    ab = np.zeros((128, 8, 384), np.float32)
    for h in range(8):
        slope = 2.0 ** (-(h + 1))
        for j in range(3):
            rel = (j - 1) * 128 + sp - tq
            ab[:, h, j * 128:(j + 1) * 128] = np.where(np.abs(rel) <= 128, -slope * np.abs(rel), -30000.0)
    out["abias"] = ab
    gm = np.zeros((128, 2, 128), np.float32)
    gm[:, 0, :] = (sp <= tq)
    gm[:, 1, :] = (sp >= tq)
    out["gmask"] = gm
    return out


def host_params(inputs, depth):
    n_even = (depth + 1) // 2
    n_odd = depth // 2
    f = lambda nm: np.asarray(inputs[nm], np.float32)
    out = {}
    lnp = np.zeros((128, 3, depth, KCD), np.float32)
    for i, nm in enumerate(("ln_ffn1", "ln_mix", "ln_ffn2")):
        lnp[:, i] = f(nm)[:depth].reshape(depth, KCD, 128).transpose(2, 0, 1)
    out["lnp"] = lnp
    evp = np.zeros((128, n_even, NEV), np.float32)
    for l in range(n_even):
        evp[:, l, 0] = f("a_q_gain")[l]
        evp[:, l, 1] = f("a_k_gain")[l]
        evp[:, l, 2:10] = f("a_sink")[l][None, :]
        cw = f("b_conv_w")[l]
        evp[:, l, 10:258] = cw.reshape(31, 8, 128).transpose(2, 1, 0).reshape(128, 248)
        evp[:, l, 258:266] = f("b_conv_b")[l].reshape(8, 128).T
        evp[:, l, 266:274] = f("b_norm_g")[l].reshape(8, 128).T
        evp[:, l, 274:282] = f("b_norm_b")[l].reshape(8, 128).T
    out["evp"] = evp
    if n_odd:
        odp = np.zeros((128, n_odd, NOD), np.float32)
        for l in range(n_odd):
            odp[:, l, 0:8] = f("c_gate_bias")[l].reshape(2, 4, 128).transpose(2, 0, 1).reshape(128, 8)
            odp[:, l, 8:10] = f("c_norm_g")[l].reshape(2, 128).T
            odp[:, l, 10:42] = f("d_conv_w")[l].reshape(4, 8, 128).transpose(2, 1, 0).reshape(128, 32)
            odp[:, l, 42:50] = f("d_conv_b")[l].reshape(8, 128).T
            odp[:, l, 50:66] = f("d_ba")[l].reshape(2, 8, 128).transpose(2, 0, 1).reshape(128, 16)
            odp[:, l, 66:82] = f("d_bx")[l].reshape(2, 8, 128).transpose(2, 0, 1).reshape(128, 16)
            odp[:, l, 82:98] = f("d_lambda")[l].reshape(2, 8, 128).transpose(2, 0, 1).reshape(128, 16)
        out["odp"] = odp
        out["gup"] = f("c_gate_up")[:n_odd]
        out["dwa"] = f("d_wa")[:n_odd]
        out["dwx"] = f("d_wx")[:n_odd]
    for nm in ("ffn1_w_in", "ffn1_w_out", "ffn2_w_in", "ffn2_w_out"):
        out[nm] = f(nm)[:depth]
    out["ev_w_in"] = f("ev_w_in")[:n_even]
    out["ev_w_out"] = f("ev_w_out")[:n_even]
    if n_odd:
        out["od_w_in"] = f("od_w_in")[:n_odd]
        out["od_w_out"] = f("od_w_out")[:n_odd]
    return out


def build_program(cfg):
    b = Builder(cfg)
    nc = b.build()
    return nc, b


_CACHE = {}


def kernel(**inputs):
    depth = 4
    cfg = {"NTOK": 4096, "SEG": 2048, "DEPTH": depth}
    nc, b = build_program(cfg)
    shared = host_params(inputs, depth)
    shared.update(host_consts())
    xp = np.asarray(inputs["x_prompt"], np.float32)
    xs = np.asarray(inputs["x_sample"], np.float32)
    in_maps = []
    for c in range(8):
        m = dict(shared)
        if c < 4:
            m["xT"] = np.ascontiguousarray(xp[2 * c:2 * c + 2].reshape(4096, D_MODEL).T)
            m["carry"] = np.zeros((128, 1), np.float32)
        else:
            m["xT"] = np.ascontiguousarray(xs[c - 4].T)
            m["carry"] = np.ones((128, 1), np.float32)
        in_maps.append(m)
    res = run_bass_kernel_spmd(nc, in_maps, core_ids=list(range(8)))
    yp = np.empty_like(xp)
    ys = np.empty_like(xs)
    for c in range(8):
        y = res.results[c]["yT"].T
        if c < 4:
            yp[2 * c:2 * c + 2] = y.reshape(2, 2048, D_MODEL)
        else:
            ys[c - 4] = y
    return (yp, ys)
```

```python
import numpy as np
from contextlib import ExitStack
import concourse.bass as bass
import concourse.mybir as mybir
from concourse.bass_utils import run_bass_kernel_spmd

F32 = mybir.dt.float32
BF16 = mybir.dt.bfloat16
AF = mybir.ActivationFunctionType
ALU = mybir.AluOpType

D_MODEL = 4096
KCD = D_MODEL // 128
D_FF = 1536
EPS = 1e-6
EVEN_IN = 3584
ODD_IN = 5152
TT = 512
ENGS = ("pe", "act", "dve", "pool", "sp")
BLOCK_ATTR = {"pe": "tensor", "act": "scalar", "dve": "vector", "pool": "gpsimd", "sp": "sync"}
SAME_ENG_WAIT = True


class Res:
    __slots__ = ("name", "w", "r")

    def __init__(self, name):
        self.name = name
        self.w = None
        self.r = []


class DSem:
    __slots__ = ("h", "count", "bar", "sw")

    def __init__(self, h):
        self.h = h
        self.count = 0
        self.bar = 0
        self.sw = False


class Op:
    __slots__ = ("eng", "fn", "deps", "needed", "sigval", "pairs", "dsem", "epoch")

    def __init__(self, eng, fn, epoch):
        self.eng = eng
        self.fn = fn
        self.deps = []
        self.needed = False
        self.sigval = None
        self.pairs = None
        self.dsem = None
        self.epoch = epoch


class Sched:
    def __init__(self, nc, es):
        self.nc = nc
        self.es = es
        self.eng = {"pe": nc.tensor, "act": nc.scalar, "dve": nc.vector, "pool": nc.gpsimd, "sp": nc.sync}
        self.prog = {e: es.enter_context(nc.semaphore("prog_" + e)) for e in ("pe", "act", "dve", "pool")}
        self.sigcount = {e: 0 for e in self.prog}
        self.waited = {e: {} for e in ENGS}
        self.ops = {e: [] for e in ENGS}
        self.lastc = {e: None for e in ENGS}
        self.dsems = []
        self.free_dsems = []
        self.free_sw = []
        self.epoch = 0
        self.nsem = 0

    def dsem(self, sw=False):
        fl = self.free_sw if sw else self.free_dsems
        if fl:
            return fl.pop()
        h = self.es.enter_context(self.nc.semaphore("dsem%d" % self.nsem))
        self.nsem += 1
        d = DSem(h)
        d.sw = sw
        self.dsems.append(d)
        return d

    def release(self, *ds):
        for d in ds:
            (self.free_sw if d.sw else self.free_dsems).append(d)

    def _collect(self, eng, reads, writes):
        deps = []
        for r in reads:
            if r.w is not None:
                deps.append(r.w)
        for w in writes:
            if w.w is not None:
                deps.append(w.w)
            deps.extend(w.r)
        out = []
        seen = set()
        for d in deps:
            if isinstance(d, Op):
                if d.epoch < self.epoch:
                    continue
                if d.eng == eng and (eng in ("pe", "sp") or not SAME_ENG_WAIT):
                    continue
                if id(d) in seen:
                    continue
                seen.add(id(d))
                d.needed = True
                out.append(d)
            else:
                ds, val = d
                if val <= ds.bar:
                    continue
                key = (id(ds), val)
                if key in seen:
                    continue
                seen.add(key)
                out.append(d)
        return out

    def op(self, eng, fn, reads=(), writes=()):
        o = Op(eng, fn, self.epoch)
        o.deps = self._collect(eng, reads, writes)
        for r in reads:
            r.r.append(o)
        for w in writes:
            w.w = o
            w.r = []
        self.ops[eng].append(o)
        self.lastc[eng] = o
        return o

    def dma(self, q, pairs, reads, writes, dsem):
        assert dsem.sw == (q == "pool"), "dma semaphore kind mismatch"
        o = Op(q, None, self.epoch)
        o.pairs = pairs
        o.dsem = dsem
        o.deps = self._collect(q, reads, writes)
        dsem.count += 16 * len(pairs)
        tok = (dsem, dsem.count)
        for r in reads:
            r.r.append(tok)
        for w in writes:
            w.w = tok
            w.r = []
        self.ops[q].append(o)
        return o

    def barrier(self):
        toks = [(d, d.count) for d in self.dsems if d.count > d.bar]
        for e in ENGS:
            o = Op(e, None, self.epoch)
            deps = []
            for x in ("pe", "act", "dve", "pool"):
                if x != e and self.lastc[x] is not None and self.lastc[x].epoch == self.epoch:
                    self.lastc[x].needed = True
                    deps.append(self.lastc[x])
            deps.extend(toks)
            o.deps = deps
            self.ops[e].append(o)
        for d in self.dsems:
            d.bar = d.count
        self.epoch += 1

    def flush(self):
        for e in self.prog:
            c = self.sigcount[e]
            for o in self.ops[e]:
                if o.fn is not None and o.needed:
                    c += 1
                    o.sigval = c
            self.sigcount[e] = c
        with self.nc.Block() as block:
            for ename in ENGS:
                ops = self.ops[ename]
                if not ops:
                    continue

                def body(e, ename=ename, ops=ops):
                    waited = self.waited[ename]
                    for o in ops:
                        need = {}
                        for d in o.deps:
                            if isinstance(d, Op):
                                sem = self.prog[d.eng]
                                key = "p" + d.eng
                                val = d.sigval
                            else:
                                sem = d[0].h
                                key = id(d[0])
                                val = d[1]
                            if waited.get(key, 0) >= val:
                                continue
                            if key not in need or need[key][1] < val:
                                need[key] = (sem, val)
                        for key, (sem, val) in need.items():
                            e.wait_ge(sem, val)
                            waited[key] = val
                        if o.fn is not None:
                            ins = o.fn(e)
                            if o.needed:
                                ins.then_inc(self.prog[ename], 1)
                        elif o.pairs is not None:
                            for (oa, ia) in o.pairs:
                                e.dma_start(out=oa, in_=ia).then_inc(o.dsem.h, 16)

                getattr(block, BLOCK_ATTR[ename])(body)
        self.ops = {e: [] for e in ENGS}


_UID = [0]


def uname(name):
    _UID[0] += 1
    return "t%d_%s" % (_UID[0], name)


class Rot:
    def __init__(self, items):
        self.items = items
        self.i = 0

    def next(self):
        it = self.items[self.i % len(self.items)]
        self.i += 1
        return it


def weight_specs(depth):
    sp = []
    for l in range(depth):
        sp.append((("f1i", l), "ffn1_w_in", l, D_MODEL, 2 * D_FF))
        sp.append((("f1o", l), "ffn1_w_out", l, D_FF, D_MODEL))
        sp.append((("f2i", l), "ffn2_w_in", l, D_MODEL, 2 * D_FF))
        sp.append((("f2o", l), "ffn2_w_out", l, D_FF, D_MODEL))
        if l % 2 == 0:
            sp.append((("mi", l), "ev_w_in", l // 2, D_MODEL, EVEN_IN))
            sp.append((("mo", l), "ev_w_out", l // 2, 2048, D_MODEL))
        else:
            sp.append((("mi", l), "od_w_in", l // 2, D_MODEL, ODD_IN))
            sp.append((("mo", l), "od_w_out", l // 2, 2048, D_MODEL))
    return sp


class Builder:
    def __init__(self, cfg):
        self.cfg = cfg
        self.NTOK = cfg["NTOK"]
        self.SEG = cfg["SEG"]
        self.DEPTH = cfg["DEPTH"]
        self.NT = self.NTOK // TT
        self.stop_after = cfg.get("stop_after")
        self.debug = cfg.get("debug", False)
        self.nc = bass.Bass("TRN2", target_bir_lowering=False)
        self.es = ExitStack()

    def declare(self):
        nc, NTOK, DEPTH = self.nc, self.NTOK, self.DEPTH
        n_even = (DEPTH + 1) // 2
        n_odd = DEPTH // 2
        I = {}

        def inp(name, shape, dt=F32):
            I[name] = nc.dram_tensor(name, list(shape), dt, kind="ExternalInput").ap()

        inp("xT", [D_MODEL, NTOK])
        inp("carry", [128, 1])
        inp("ffn1_w_in", [DEPTH, D_MODEL, 2 * D_FF])
        inp("ffn1_w_out", [DEPTH, D_FF, D_MODEL])
        inp("ffn2_w_in", [DEPTH, D_MODEL, 2 * D_FF])
        inp("ffn2_w_out", [DEPTH, D_FF, D_MODEL])
        inp("ev_w_in", [n_even, D_MODEL, EVEN_IN])
        inp("ev_w_out", [n_even, 2048, D_MODEL])
        if n_odd:
            inp("od_w_in", [n_odd, D_MODEL, ODD_IN])
            inp("od_w_out", [n_odd, 2048, D_MODEL])
        inp("lnp", [128, 3, DEPTH, KCD])
        inp("ident", [128, 128])
        inp("abias", [128, 8, 384])
        inp("evp", [128, n_even, 2 + 8 + 8 * 31 + 8 * 3])
        if n_odd:
            inp("odp", [128, n_odd, NOD])
            inp("gup", [n_odd, 2, 16, 512])
            inp("dwa", [n_odd, 2, 8, 128, 128])
            inp("dwx", [n_odd, 2, 8, 128, 128])
            inp("gmask", [128, 2, 128])
        self.I = I
        kind = "ExternalOutput"
        self.yT = nc.dram_tensor("yT", [D_MODEL, NTOK], F32, kind=kind).ap()
        dk = "ExternalOutput" if self.debug else "Internal"
        self.P = nc.dram_tensor("Pscr", [ODD_IN, NTOK], BF16, kind=dk).ap()
        self.M = nc.dram_tensor("Mscr", [2048, NTOK], BF16, kind=dk).ap()
        self.Wb = {}
        for key, name, li, K, N in weight_specs(DEPTH):
            nch = (N + 127) // 128
            self.Wb[key] = (nc.dram_tensor("wb_%s_%d" % key, [nch, 128, K], BF16, kind="Internal").ap(), K, N, nch)

    def build(self):
        nc, es = self.nc, self.es
        self.declare()
        with es:
            self.S = Sched(nc, es)
            self.consts()
            self.prepass()
            self.run_phases()
        return nc

    def consts(self):
        nc, es, S, I = self.nc, self.es, self.S, self.I
        DEPTH = self.DEPTH

        def sb(name, shape, dt):
            return es.enter_context(nc.sbuf_tensor(uname(name), list(shape), dt))

        self.lnp = sb("lnp", [128, 3, DEPTH, KCD], F32)
        self.carry = sb("carry", [128, 1], F32)
        self.ones_bf = sb("ones_bf", [128, 128], BF16)
        self.ones_f = sb("ones_f", [128, 128], F32)
        self.ident_f = sb("ident_f", [128, 128], F32)
        self.ident_bf = sb("ident_bf", [128, 128], BF16)
        d = S.dsem()
        r = Res("consts")
        S.dma("sp", [(self.lnp[:], I["lnp"][:, :, :, :]), (self.carry[:], I["carry"][:, :]),
                     (self.ident_f[:], I["ident"][:, :])], [], [r], d)
        S.op("dve", lambda e: e.memset(self.ones_bf[:], 1.0), [], [r])
        S.op("dve", lambda e: e.memset(self.ones_f[:], 1.0), [], [r])
        self.eps_col = sb("eps_col", [128, 1], F32)
        self.one_col = sb("one_col", [128, 1], F32)
        S.op("dve", lambda e: e.memset(self.one_col[:], 1.0), [], [r])
        S.op("dve", lambda e: e.memset(self.eps_col[:], EPS), [], [r])
        S.op("dve", lambda e: e.tensor_copy(out=self.ident_bf[:], in_=self.ident_f[:]), [r], [r])
        S.barrier()
        S.flush()
        S.release(d)

    def prepass(self):
        with ExitStack() as es:
            rel = self.emit_prepass(es, [("f1i", 0), ("f1o", 0), ("mi", 0)], bg=False)
            self.S.barrier()
            self.S.flush()
            self.S.release(*rel)

    def bg_keys(self, k, part):
        if part == 0:
            return [("mo", k), ("f2i", k), ("f2o", k)]
        if k + 1 < self.DEPTH:
            return [("f1i", k + 1), ("f1o", k + 1), ("mi", k + 1)]
        return []

    def emit_prepass(self, es, keys, bg, NB=3):
        nc, S, I = self.nc, self.S, self.I
        if not keys:
            return []
        KG = 4 if bg else 8
        if self.cfg.get("verbose"):
            print("prepass", keys, "sbuf remaining", nc.sbuf_bytes_remaining)
        stf = [es.enter_context(nc.sbuf_tensor(uname("stf"), [128, KG, 512], F32)) for i in range(NB)]
        stb = [es.enter_context(nc.sbuf_tensor(uname("stb"), [128, 4, KG, 128], BF16)) for i in range(NB)]
        rf = [Res("stf%d" % i) for i in range(NB)]
        rb = [Res("stb%d" % i) for i in range(NB)]
        dl = [S.dsem(bg) for _ in range(NB)]
        dst = [S.dsem(bg) for _ in range(NB)]
        specs = {key: (name, li) for key, name, li, K, N in weight_specs(self.DEPTH)}
        items = []
        for key in keys:
            name, li = specs[key]
            wb, K, N, nch = self.Wb[key]
            w = I[name][li]
            kgs = [(k0, min(KG, K // 128 - k0)) for k0 in range(0, K // 128, KG)]
            for n0 in range(0, N, 512):
                ncol = min(512, N - n0)
                for (k0, kn) in kgs:
                    items.append((w, wb, n0, ncol, k0, kn))
        cast_engs = ("pool",) if bg else ("dve", "pool", "act")
        lq = "pool" if bg else "sp"
        sq_ = "pool" if bg else "act"

        def load(i):
            w, wb, n0, ncol, k0, kn = items[i]
            s = i % NB
            src = w[k0 * 128:(k0 + kn) * 128, n0:n0 + ncol].rearrange("(kc p) n -> p kc n", p=128)
            S.dma(lq, [(stf[s][:, 0:kn, 0:ncol], src)], [], [rf[s]], dl[s])

        def cast_store(i):
            w, wb, n0, ncol, k0, kn = items[i]
            s = i % NB
            ce = cast_engs[i % len(cast_engs)]
            if ncol == 512:
                oa = stb[s][:, :, 0:kn, :].rearrange("p nn kc j -> p kc nn j")
                ia = stf[s][:, 0:kn, :].rearrange("p kc (nn j) -> p kc nn j", j=128)
                ncc = 4
            else:
                assert ncol < 128
                oa = stb[s][:, 0, 0:kn, 0:ncol]
                ia = stf[s][:, 0:kn, 0:ncol]
                ncc = 1
            if ce == "act":
                S.op("act", lambda e, oa=oa, ia=ia: e.copy(out=oa, in_=ia), [rf[s]], [rb[s]])
            else:
                S.op(ce, lambda e, oa=oa, ia=ia: e.tensor_copy(out=oa, in_=ia), [rf[s]], [rb[s]])
            c0 = n0 // 128
            dsta = wb[c0:c0 + ncc, :, k0 * 128:(k0 + kn) * 128].rearrange("c p x -> p c x")
            srca = stb[s][:, 0:ncc, 0:kn, :].rearrange("p c kc j -> p c (kc j)")
            S.dma(sq_, [(dsta, srca)], [rb[s]], [], dst[s])

        n = len(items)
        if bg:
            la = NB - 1
            steps = []
            for i in range(n):
                def step(i=i):
                    if i == 0:
                        for j in range(min(la, n)):
                            load(j)
                    if i + la < n:
                        load(i + la)
                    cast_store(i)
                steps.append(step)
            self.bg_steps = steps
        else:
            for i in range(n):
                load(i)
                cast_store(i)
        return dl + dst

    def phase_list(self):
        ph = []
        for k in range(self.DEPTH + 1):
            ph.append(("A", k))
            if k < self.DEPTH:
                ph.append(("B", k))
        if self.stop_after is not None:
            idx = ph.index(tuple(self.stop_after))
            ph = ph[:idx + 1]
        return ph

    def run_phases(self):
        for kind, k in self.phase_list():
            if kind == "A":
                self.phase_A(k)
            elif k % 2 == 0:
                self.phase_B_even(k)
            else:
                self.phase_B_odd(k)

    def phase_A(self, k):
        nc, S, I = self.nc, self.S, self.I
        DEPTH = self.DEPTH
        do_pre = k > 0
        do_post = k < DEPTH
        with ExitStack() as es:
            def sb(name, shape, dt):
                return es.enter_context(nc.sbuf_tensor(uname(name), list(shape), dt))

            X = sb("X", [128, KCD, TT], F32)
            XG = sb("XG", [128, KCD, TT], BF16)
            HID = sb("HID", [128, 12, TT], BF16)
            MT = sb("MT", [128, 16, TT], BF16) if do_pre else None
            NW = 4
            wsl = [sb("wsl%d" % i, [128, KCD, 128], BF16) for i in range(NW)]
            wrot = Rot([(wsl[i], Res("wsl%d" % i), S.dsem()) for i in range(NW)])
            sqrot = Rot([(sb("sq", [128, TT], BF16), Res("sq")) for i in range(5)])
            sgrot = Rot([(sb("sg", [128, TT], F32), Res("sg")) for i in range(2)])
            turot = Rot([(sb("tu", [128, TT], F32), Res("tu")) for i in range(2)])
            stgrot = Rot([(sb("stg", [128, TT], BF16), Res("stg"), S.dsem()) for i in range(4)])
            rinv = sb("rinv", [128, TT], F32)
            r_rinv = Res("rinv")
            ps = [es.enter_context(nc.psum_tensor(uname("psA"), [128, TT], F32)) for i in range(7)]
            psrot = Rot([(ps[i], Res("psA%d" % i)) for i in range(6)])
            psn, r_psn = ps[6], Res("psn")
            rX = [Res("X%d" % i) for i in range(KCD)]
            rXG = [Res("XG%d" % i) for i in range(KCD)]
            rHID = [Res("HID%d" % i) for i in range(12)]
            rMT = Res("MT")
            d_xld, d_xst, d_mld = S.dsem(True), S.dsem(True), S.dsem(True)
            xsrc = I["xT"] if k == 0 else self.yT

            pend = []

            def norm_chunk(kc, lnidx, layer, lag=3):
                sqa, sqr = sqrot.next()
                S.op("act", lambda e, sqa=sqa, kc=kc: e.activation(out=sqa[:], in_=X[:, kc, :], func=AF.Square),
                     [rX[kc]], [sqr])
                pend.append((sqa, sqr, kc))
                S.op("dve", lambda e, kc=kc: e.tensor_scalar(
                    out=XG[:, kc, :], in0=X[:, kc, :], scalar1=self.lnp[:, lnidx, layer, kc:kc + 1], scalar2=None,
                    op0=ALU.mult), [rX[kc]], [rXG[kc]])
                while len(pend) > lag:
                    norm_pe()

            def norm_pe():
                sqa, sqr, kc = pend.pop(0)
                S.op("pe", lambda e, sqa=sqa, kc=kc: e.matmul(psn[:], lhsT=self.ones_bf[:], rhs=sqa[:],
                                                              start=(kc == 0), stop=(kc == KCD - 1)), [sqr], [r_psn])

            def norm_finish():
                while pend:
                    norm_pe()
                S.op("act", lambda e: e.activation(out=rinv[:], in_=psn[:], func=AF.Sqrt, scale=1.0 / D_MODEL,
                                                   bias=self.eps_col[:]), [r_psn], [r_rinv])
                S.op("dve", lambda e: e.reciprocal(out=rinv[:], in_=rinv[:]), [r_rinv], [r_rinv])

            def mm_chunk(wkey, ci, act, ract, KC):
                wb, K, N, nch = self.Wb[wkey]
                M = min(128, N - ci * 128)
                wa, wr, wd = wrot.next()
                S.dma("sp", [(wa[:, 0:KC, :], wb[ci].rearrange("p (kc j) -> p kc j", j=128))], [], [wr], wd)
                pa, pr = psrot.next()

                def fn(e, wa=wa, pa=pa, M=M):
                    ins = None
                    for kc in range(KC):
                        ins = e.matmul(pa[0:M, :], lhsT=wa[:, kc, 0:M], rhs=act[:, kc, :],
                                       start=(kc == 0), stop=(kc == KC - 1))
                    return ins
                S.op("pe", fn, [wr] + list(ract), [pr])
                return pa, pr, M

            def resid(wkey, act, ract, KC, scale, next_norm):
                for ci in range(KCD):
                    pa, pr, M = mm_chunk(wkey, ci, act, ract, KC)
                    S.op("dve", lambda e, pa=pa, ci=ci: e.scalar_tensor_tensor(
                        out=X[:, ci, :], in0=pa[:], scalar=float(scale), in1=X[:, ci, :],
                        op0=ALU.mult, op1=ALU.add), [pr, rX[ci]], [rX[ci]])
                    if next_norm is not None:
                        norm_chunk(ci, *next_norm)
                if next_norm is not None:
                    norm_finish()

            def ffn(which, layer, next_norm):
                wi = ("f%di" % which, layer)
                wo = ("f%do" % which, layer)
                for n in range(12):
                    pg, prg, _ = mm_chunk(wi, n, XG, rXG, KCD)
                    pu, pru, _ = mm_chunk(wi, n + 12, XG, rXG, KCD)
                    sga, sgr = sgrot.next()
                    tua, tur = turot.next()
                    S.op("dve", lambda e, sga=sga, pg=pg: e.tensor_tensor(out=sga[:], in0=pg[:], in1=rinv[:], op=ALU.mult),
                         [prg, r_rinv], [sgr])
                    S.op("act", lambda e, sga=sga: e.activation(out=sga[:], in_=sga[:], func=AF.Silu), [sgr], [sgr])
                    S.op("dve", lambda e, tua=tua, pu=pu: e.tensor_tensor(out=tua[:], in0=pu[:], in1=rinv[:], op=ALU.mult),
                         [pru, r_rinv], [tur])
                    S.op("dve", lambda e, sga=sga, tua=tua, n=n: e.tensor_tensor(
                        out=HID[:, n, :], in0=sga[:], in1=tua[:], op=ALU.mult), [sgr, tur], [rHID[n]])
                resid(wo, HID, rHID, 12, 0.5, next_norm)

            def load_tile(t):
                t0 = t * TT
                if do_pre:
                    pairs = []
                    for g in range(2):
                        pairs.append((MT[:, g * 8:(g + 1) * 8, :],
                                      self.M[g * 1024:(g + 1) * 1024, t0:t0 + TT].rearrange("(kc p) n -> p kc n", p=128)))
                    S.dma("pool", pairs, [], [rMT], d_mld)
                pairs = []
                for g in range(4):
                    pairs.append((X[:, g * 8:(g + 1) * 8, :],
                                  xsrc[g * 1024:(g + 1) * 1024, t0:t0 + TT].rearrange("(kc p) n -> p kc n", p=128)))
                S.dma("pool", pairs, [], rX, d_xld)

            def store_tile(t):
                t0 = t * TT
                pairs = []
                for g in range(4):
                    pairs.append((self.yT[g * 1024:(g + 1) * 1024, t0:t0 + TT].rearrange("(kc p) n -> p kc n", p=128),
                                  X[:, g * 8:(g + 1) * 8, :]))
                S.dma("pool", pairs, rX, [], d_xst)

            self.bg_steps = []
            prel = []
            if k < DEPTH:
                prel = self.emit_prepass(es, self.bg_keys(k, 0) + self.bg_keys(k, 1), bg=True, NB=2)
            nsteps = len(self.bg_steps)
            load_tile(0)
            for t in range(self.NT):
                t0 = t * TT
                for st in self.bg_steps[t * nsteps // self.NT:(t + 1) * nsteps // self.NT]:
                    st()
                if do_pre:
                    resid(("mo", k - 1), MT, [rMT], 16, 1.0, (2, k - 1))
                    ffn(2, k - 1, (0, k) if do_post else None)
                else:
                    for kc in range(KCD):
                        norm_chunk(kc, 0, k)
                    norm_finish()
                if do_post:
                    ffn(1, k, (1, k))
                store_tile(t)
                if t + 1 < self.NT:
                    load_tile(t + 1)
                if do_post:
                    wkey = ("mi", k)
                    nch = self.Wb[wkey][3]
                    for ci in range(nch):
                        pa, pr, M = mm_chunk(wkey, ci, XG, rXG, KCD)
                        sa, sr, sd = stgrot.next()
                        S.op("dve", lambda e, sa=sa, pa=pa, M=M: e.tensor_tensor(out=sa[0:M, :], in0=pa[0:M, :], in1=rinv[0:M, :],
                                                                                op=ALU.mult), [pr, r_rinv], [sr])
                        S.dma("act", [(self.P[ci * 128:ci * 128 + M, t0:t0 + TT], sa[0:M, :])], [sr], [], sd)
            S.barrier()
            S.flush()
            S.release(d_xld, d_xst, d_mld, *prel, *[w[2] for w in wrot.items], *[s[2] for s in stgrot.items])

    def phase_B_even(self, k):
        self.even_attention(k)
        self.even_conv(k)

    def even_attention(self, k):
        nc, S, I = self.nc, self.S, self.I
        l = k // 2
        NTOK, SEG = self.NTOK, self.SEG
        NB = NTOK // 128
        BPS = SEG // 128
        NT = NTOK // TT
        P, Mo = self.P, self.M
        with ExitStack() as es:
            def sb(name, shape, dt):
                return es.enter_context(nc.sbuf_tensor(uname(name), list(shape), dt))

            def pst(name, shape, dt):
                return es.enter_context(nc.psum_tensor(uname(name), list(shape), dt))

            evp = sb("evp", [128, NEV], F32)
            abias = sb("abias", [128, 8, 384], F32)
            esink = sb("esink", [128, 8], F32)
            QK = sb("QK", [128, 10, NTOK], BF16)
            VT = sb("VT", [128, 2, NTOK], BF16)
            VTOK = sb("VTOK", [128, NB, 2, 128], BF16)
            r_par = Res("par")
            d0, d1 = S.dsem(), S.dsem()
            prel = []
            S.dma("sp", [(evp[:], I["evp"][:, l, :]), (abias[:], I["abias"][:, :, :])], [], [r_par], d0)
            S.op("act", lambda e: e.activation(out=esink[:], in_=evp[:, 2:10], func=AF.Exp), [r_par], [r_par])
            etab = sb("etab", [128, 8, 384], BF16)
            S.op("act", lambda e: e.activation(out=etab[:], in_=abias[:], func=AF.Exp), [r_par], [r_par])
            rQK = [[Res("QK%d_%d" % (c, t)) for t in range(NT)] for c in range(10)]
            rVT = Res("VT")
            allqk = [r for row in rQK for r in row]
            S.dma("sp", [(QK[:, c, :], P[c * 128:(c + 1) * 128, :]) for c in range(10)]
                  + [(VT[:, c, :], P[1280 + c * 128:1280 + (c + 1) * 128, :]) for c in range(2)],
                  [], allqk + [rVT], d1)
            sqrot = Rot([(sb("sq", [128, TT], BF16), Res("sq")) for _ in range(6)])
            rirot = Rot([(sb("ri", [128, TT], F32), Res("ri")) for _ in range(6)])
            banks = [(pst("bkE", [128, 512], F32), Res("bkE")) for _ in range(6)]
            psnrot = Rot(list(banks))
            for c in range(10):
                gcol = evp[:, 0:1] if c < 8 else evp[:, 1:2]
                for t in range(NT):
                    sl = slice(t * TT, (t + 1) * TT)
                    sqa, sqr = sqrot.next()
                    ria, rir = rirot.next()
                    pna, pnr = psnrot.next()
                    S.op("act", lambda e, sqa=sqa, c=c, sl=sl: e.activation(out=sqa[:], in_=QK[:, c, sl], func=AF.Square),
                         [rQK[c][t]], [sqr])
                    S.op("pe", lambda e, sqa=sqa, pna=pna: e.matmul(pna[:], lhsT=self.ones_bf[:], rhs=sqa[:],
                                                                    start=True, stop=True), [sqr], [pnr])
                    S.op("act", lambda e, ria=ria, pna=pna: e.activation(out=ria[:], in_=pna[:], func=AF.Sqrt,
                                                                         scale=1.0 / 128, bias=self.eps_col[:]),
                         [pnr], [rir])
                    S.op("dve", lambda e, ria=ria: e.reciprocal(out=ria[:], in_=ria[:]), [rir], [rir])
                    S.op("dve", lambda e, ria=ria, c=c, sl=sl, gcol=gcol: e.scalar_tensor_tensor(
                        out=QK[:, c, sl], in0=QK[:, c, sl], scalar=gcol, in1=ria[:], op0=ALU.mult, op1=ALU.mult),
                        [rir, rQK[c][t]], [rQK[c][t]])
            ptrot = Rot([(pst("pt", [128, 1024], BF16)[:, 0:256].rearrange("p (a b) -> p a b", b=128), Res("pt")) for _ in range(2)])
            rVTOK = [Res("VTOK%d" % m) for m in range(NB)]
            for m in range(NB):
                pta, ptr = ptrot.next()

                def fn(e, pta=pta, m=m):
                    ins = None
                    for kv in range(2):
                        ins = e.transpose(pta[:, kv, :], VT[:, kv, m * 128:(m + 1) * 128], self.ident_bf[:])
                    return ins
                S.op("pe", fn, [rVT], [ptr])
                S.op("act", lambda e, pta=pta, m=m: e.copy(out=VTOK[:, m, :, :], in_=pta[:]), [ptr], [rVTOK[m]])
            pssrot = Rot([(banks[i][0][:, 0:384], banks[i][1]) for i in range(3)])
            psorot = Rot([(banks[3 + i][0][:, 0:256].rearrange("p (a b) -> p a b", b=128), banks[3 + i][1]) for i in range(3)])
            sbtrot = Rot([(sb("sbt", [128, 384], BF16), Res("sbt")) for _ in range(4)])
            ptrot2 = Rot([(sb("PT", [128, 384], BF16), Res("PT")) for _ in range(4)])
            denrot = Rot([(sb("den", [128, 128], F32), Res("den")) for _ in range(4)])
            stgrot = Rot([(sb("ao", [128, 8, TT], BF16), Res("ao"), S.dsem()) for _ in range(2)])
            scale = 128.0 ** -0.5
            sta = None
            for n in range(NB):
                if n % 4 == 0:
                    sta, str_, std = stgrot.next()
                js = []
                for j in range(3):
                    m = n - 1 + j
                    if 0 <= m < NB:
                        js.append((j, m, (m // BPS) != (n // BPS)))
                jlo, jhi = js[0][0], js[-1][0]
                cs = slice(jlo * 128, (jhi + 1) * 128)
                for h in range(8):
                    kv = h // 4
                    psa, psr = pssrot.next()
                    poa, por = psorot.next()
                    sba, sbr = sbtrot.next()
                    pta, ptr = ptrot2.next()
                    dna, dnr = denrot.next()

                    def fn(e, psa=psa, js=js, kv=kv, h=h, n=n):
                        ins = None
                        for (j, m, cross) in js:
                            ins = e.matmul(psa[:, j * 128:(j + 1) * 128], lhsT=QK[:, 8 + kv, m * 128:(m + 1) * 128],
                                           rhs=QK[:, h, n * 128:(n + 1) * 128], start=True, stop=True)
                        return ins
                    rk = list({id(r): r for r in [rQK[8 + kv][m // 4] for (_, m, _) in js]}.values())
                    S.op("pe", fn, rk + [rQK[h][n // 4]], [psr])
                    S.op("act", lambda e, sba=sba, psa=psa, cs=cs: e.activation(out=sba[:, cs], in_=psa[:, cs], func=AF.Exp,
                                                                                scale=scale), [psr], [sbr])
                    S.op("dve", lambda e, sba=sba, pta=pta, h=h, cs=cs: e.tensor_tensor(
                        out=pta[:, cs], in0=sba[:, cs], in1=etab[:, h, cs], op=ALU.mult), [sbr, r_par], [ptr])
                    for (j, m, cross) in js:
                        if cross:
                            S.op("dve", lambda e, pta=pta, j=j: e.tensor_scalar(
                                out=pta[:, j * 128:(j + 1) * 128], in0=pta[:, j * 128:(j + 1) * 128],
                                scalar1=self.carry[:, 0:1], scalar2=None, op0=ALU.mult), [ptr], [ptr])

                    def fn2(e, poa=poa, pta=pta, js=js, kv=kv):
                        ins = None
                        for i, (j, m, cross) in enumerate(js):
                            ins = e.matmul(poa[:, 0, :], lhsT=VTOK[:, m, kv, :], rhs=pta[:, j * 128:(j + 1) * 128],
                                           start=(i == 0), stop=(i == len(js) - 1))
                        for i, (j, m, cross) in enumerate(js):
                            ins = e.matmul(poa[:, 1, :], lhsT=self.ones_bf[:], rhs=pta[:, j * 128:(j + 1) * 128],
                                           start=(i == 0), stop=(i == len(js) - 1))
                        return ins
                    S.op("pe", fn2, [ptr] + [rVTOK[m] for (_, m, _) in js], [por])
                    S.op("dve", lambda e, dna=dna, poa=poa, h=h: e.tensor_scalar(
                        out=dna[:], in0=poa[:, 1, :], scalar1=esink[:, h:h + 1], scalar2=None, op0=ALU.add),
                        [por, r_par], [dnr])
                    S.op("dve", lambda e, dna=dna: e.reciprocal(out=dna[:], in_=dna[:]), [dnr], [dnr])
                    q4 = n % 4
                    S.op("dve", lambda e, dna=dna, poa=poa, sta=sta, h=h, q4=q4: e.tensor_tensor(
                        out=sta[:, h, q4 * 128:(q4 + 1) * 128], in0=poa[:, 0, :], in1=dna[:], op=ALU.mult),
                        [por, dnr], [str_])
                if n % 4 == 3:
                    t0 = (n - 3) * 128
                    S.dma("sp", [(Mo[0:1024, t0:t0 + TT].rearrange("(h p) n -> p h n", p=128), sta[:])], [str_], [], std)
            S.barrier()
            S.flush()
            S.release(d0, d1, *prel, *[x[2] for x in stgrot.items])

    def even_conv(self, k):
        nc, S, I = self.nc, self.S, self.I
        l = k // 2
        NTOK, SEG = self.NTOK, self.SEG
        NSEG = NTOK // SEG
        SEGP = SEG + 30
        TT = min(512, SEG)
        P, Mo = self.P, self.M
        with ExitStack() as es:
            def sb(name, shape, dt):
                return es.enter_context(nc.sbuf_tensor(uname(name), list(shape), dt))

            def pst(name, shape, dt):
                return es.enter_context(nc.psum_tensor(uname(name), list(shape), dt))

            evp = sb("evp", [128, NEV], F32)
            r_par = Res("par")
            d0 = S.dsem()
            prel = []
            S.dma("sp", [(evp[:], I["evp"][:, l, :])], [], [r_par], d0)
            CV = sb("CV", [128, 8, SEG], F32)
            rCV = [Res("CV%d" % c) for c in range(8)]
            ugrot = Rot([(sb("U", [128, SEGP], BF16), sb("G", [128, SEGP], BF16), Res("UG"), S.dsem()) for _ in range(2)])
            sigrot = Rot([(sb("sig", [128, SEGP], F32), Res("sig")) for _ in range(2)])
            glurot = Rot([(sb("glu", [128, SEGP], BF16), Res("glu")) for _ in range(2)])
            dgrot = Rot([(sb("DG", [128, 31, 128], BF16), Res("DG")) for _ in range(2)])
            pcrot = Rot([(pst("pc", [128, 512], F32)[:, 0:TT], Res("pc")) for _ in range(3)])
            mu = sb("mu", [128, TT], F32)
            var = sb("var", [128, TT], F32)
            rstd = sb("rstd", [128, TT], F32)
            r_mu, r_var, r_rstd = Res("mu"), Res("var"), Res("rstd")
            sqcrot = Rot([(sb("sqc", [128, TT], F32), Res("sqc")) for _ in range(2)])
            tmprot = Rot([(sb("tmp", [128, TT], F32), Res("tmp")) for _ in range(2)])
            stgrot = Rot([(sb("co", [128, 8, TT], BF16), Res("co"), S.dsem()) for _ in range(2)])
            ps1, ps2 = pst("ps1", [128, 512], F32)[:, 0:TT], pst("ps2", [128, 512], F32)[:, 0:TT]
            r_ps1, r_ps2 = Res("ps1"), Res("ps2")
            for s in range(NSEG):
                g0 = s * SEG - 15
                lo = max(0, g0)
                hi = min(NTOK, (s + 1) * SEG + 15)
                o_lo, o_hi = lo - g0, hi - g0
                for c in range(8):
                    ua, ga, ugr, ugd = ugrot.next()
                    row = 1536 + c * 128
                    S.dma("sp", [(ua[:, o_lo:o_hi], P[row:row + 128, lo:hi]),
                                 (ga[:, o_lo:o_hi], P[row + 1024:row + 1024 + 128, lo:hi])], [], [ugr], ugd)
                    sga, sgr = sigrot.next()
                    gla, glr = glurot.next()
                    ce = "dve"
                    S.op("act", lambda e, sga=sga, ga=ga, o_lo=o_lo, o_hi=o_hi: e.activation(out=sga[:, o_lo:o_hi], in_=ga[:, o_lo:o_hi],
                                                                       func=AF.Sigmoid), [ugr], [sgr])
                    if o_lo > 0:
                        S.op(ce, lambda e, gla=gla, o_lo=o_lo: e.memset(gla[:, 0:o_lo], 0.0), [], [glr])
                    if o_hi < SEGP:
                        S.op(ce, lambda e, gla=gla, o_hi=o_hi: e.memset(gla[:, o_hi:SEGP], 0.0), [], [glr])
                    S.op(ce, lambda e, gla=gla, sga=sga, ua=ua, o_lo=o_lo, o_hi=o_hi: e.tensor_tensor(
                        out=gla[:, o_lo:o_hi], in0=ua[:, o_lo:o_hi], in1=sga[:, o_lo:o_hi], op=ALU.mult),
                        [ugr, sgr], [glr])
                    if s > 0:
                        S.op(ce, lambda e, gla=gla: e.tensor_scalar(out=gla[:, 0:15], in0=gla[:, 0:15],
                                                                     scalar1=self.carry[:, 0:1], scalar2=None,
                                                                     op0=ALU.mult), [glr], [glr])
                    if s < NSEG - 1:
                        S.op(ce, lambda e, gla=gla: e.tensor_scalar(out=gla[:, SEG + 15:SEGP], in0=gla[:, SEG + 15:SEGP],
                                                                     scalar1=self.carry[:, 0:1], scalar2=None,
                                                                     op0=ALU.mult), [glr], [glr])
                    wc = 10 + c * 31
                    dga, dgr = dgrot.next()
                    for j in range(31):
                        S.op("dve", lambda e, dga=dga, j=j, wc=wc: e.tensor_scalar(
                            out=dga[:, j, :], in0=self.ident_bf[:], scalar1=evp[:, wc + j:wc + j + 1], scalar2=None,
                            op0=ALU.mult), [r_par], [dgr])
                    for t in range(SEG // TT):
                        pca, pcr = pcrot.next()

                        def fnc(e, pca=pca, dga=dga, gla=gla, t=t):
                            ins = None
                            for j in range(31):
                                ins = e.matmul(pca, lhsT=dga[:, j, :], rhs=gla[:, t * TT + j:t * TT + j + TT],
                                               start=(j == 0), stop=(j == 30))
                            return ins
                        S.op("pe", fnc, [dgr, glr], [pcr])
                        S.op("act", lambda e, pca=pca, c=c, t=t: e.activation(
                            out=CV[:, c, t * TT:(t + 1) * TT], in_=pca, func=AF.Identity, bias=evp[:, 258 + c:259 + c]),
                            [pcr, r_par], [rCV[c]])
                for t in range(SEG // TT):
                    sl = slice(t * TT, (t + 1) * TT)
                    t0 = s * SEG + t * TT
                    for c in range(8):
                        S.op("pe", lambda e, c=c, sl=sl: e.matmul(ps1[:], lhsT=self.ones_f[:], rhs=CV[:, c, sl],
                                                                  start=(c == 0), stop=(c == 7)), [rCV[c]], [r_ps1])
                    for c in range(8):
                        qa, qr = sqcrot.next()
                        S.op("act", lambda e, qa=qa, c=c, sl=sl: e.activation(out=qa[:], in_=CV[:, c, sl], func=AF.Square),
                             [rCV[c]], [qr])
                        S.op("pe", lambda e, qa=qa, c=c: e.matmul(ps2[:], lhsT=self.ones_f[:], rhs=qa[:],
                                                                  start=(c == 0), stop=(c == 7)), [qr], [r_ps2])
                    S.op("dve", lambda e: e.tensor_scalar(out=mu[:], in0=ps1[:], scalar1=1.0 / 1024, scalar2=None,
                                                          op0=ALU.mult), [r_ps1], [r_mu])
                    S.op("dve", lambda e: e.tensor_tensor(out=var[:], in0=mu[:], in1=mu[:], op=ALU.mult), [r_mu], [r_var])
                    S.op("dve", lambda e: e.scalar_tensor_tensor(out=var[:], in0=ps2[:], scalar=1.0 / 1024, in1=var[:],
                                                                 op0=ALU.mult, op1=ALU.subtract), [r_ps2, r_var], [r_var])
                    S.op("act", lambda e: e.activation(out=rstd[:], in_=var[:], func=AF.Sqrt, bias=self.eps_col[:]),
                         [r_var], [r_rstd])
                    S.op("dve", lambda e: e.reciprocal(out=rstd[:], in_=rstd[:]), [r_rstd], [r_rstd])
                    sta, str_, std = stgrot.next()
                    for c in range(8):
                        ta, tr = tmprot.next()
                        S.op("dve", lambda e, ta=ta, c=c, sl=sl: e.tensor_tensor(out=ta[:], in0=CV[:, c, sl], in1=mu[:],
                                                                                 op=ALU.subtract), [rCV[c], r_mu], [tr])
                        S.op("dve", lambda e, ta=ta: e.tensor_tensor(out=ta[:], in0=ta[:], in1=rstd[:], op=ALU.mult),
                             [tr, r_rstd], [tr])
                        S.op("act", lambda e, ta=ta, sta=sta, c=c: e.activation(
                            out=sta[:, c, :], in_=ta[:], func=AF.Silu, scale=evp[:, 266 + c:267 + c],
                            bias=evp[:, 274 + c:275 + c]), [tr, r_par], [str_])
                    S.dma("sp", [(Mo[1024:2048, t0:t0 + TT].rearrange("(c p) n -> p c n", p=128), sta[:])], [str_], [], std)
            S.barrier()
            S.flush()
            S.release(d0, *prel, *[x[3] for x in ugrot.items], *[x[2] for x in stgrot.items])

    def phase_B_odd(self, k):
        parts = self.cfg.get("odd_parts", ("gla", "lru"))
        if "gla" in parts:
            self.odd_gla(k)
        if "lru" in parts:
            self.odd_lru(k)

    def odd_gla(self, k):
        nc, S, I = self.nc, self.S, self.I
        l = k // 2
        NTOK, SEG = self.NTOK, self.SEG
        NSEG = NTOK // SEG
        NCH = NTOK // 128
        NCS = SEG // 128
        LT = min(512, SEG)
        P, Mo = self.P, self.M
        with ExitStack() as es:
            def sb(name, shape, dt):
                return es.enter_context(nc.sbuf_tensor(uname(name), list(shape), dt))

            def pst(name, shape, dt):
                return es.enter_context(nc.psum_tensor(uname(name), list(shape), dt))

            odp = sb("odp", [128, NOD], F32)
            negb = sb("negb", [128, 8], F32)
            gmask = sb("gmask", [128, 2, 128], F32)
            guf = sb("guf", [16, 2, 512], F32)
            gub = sb("gub", [16, 2, 512], BF16)
            lrf = sb("lrf", [16, NTOK], BF16)
            lrb = sb("lrb", [16, NTOK], BF16)
            RMF = sb("RMF", [128, SEG], F32)
            RMB = sb("RMB", [128, SEG], F32)
            r_par = Res("par")
            r_lr = Res("lr")
            d0 = S.dsem()
            S.dma("sp", [(odp[:], I["odp"][:, l, :]), (gmask[:], I["gmask"][:, :, :]),
                         (guf[:], I["gup"][l].rearrange("d r n -> r d n")),
                         (lrf[:], P[3072:3088, :]), (lrb[:], P[3088:3104, :])], [], [r_par, r_lr], d0)
            S.op("dve", lambda e: e.tensor_scalar(out=negb[:], in0=odp[:, 0:8], scalar1=-1.0, scalar2=None, op0=ALU.mult),
                 [r_par], [r_par])
            S.op("dve", lambda e: e.tensor_copy(out=gub[:], in_=guf[:]), [r_par], [r_par])
            S.op("pool", lambda e: e.memset(RMF[:], 1.0), [], [r_par])
            S.op("pool", lambda e: e.memset(RMF[:].rearrange("p (c j) -> p c j", j=128)[:, :, 0:1], 0.0), [r_par], [r_par])
            S.op("pool", lambda e: e.memset(RMB[:], 1.0), [], [r_par])
            S.op("pool", lambda e: e.memset(RMB[:].rearrange("p (c j) -> p c j", j=128)[:, :, 127:128], 0.0), [r_par], [r_par])
            prel = []
            qT = sb("qT", [128, NTOK], BF16)
            kT = sb("kT", [128, NTOK], BF16)
            vT = sb("vT", [128, 2, NTOK], BF16)
            VTOK = sb("VTOKg", [128, NCH, 256], BF16)
            SP = sb("SP", [128, SEG], F32)
            BC = sb("BC", [128, SEG], F32)
            EX = sb("EX", [128, SEG], F32)
            qin = sb("qin", [128, SEG], BF16)
            kin = sb("kin", [128, SEG], BF16)
            kout = sb("kout", [128, SEG], BF16)
            DEC = sb("DEC", [128, NCS], F32)
            OF = sb("OF", [128, 2, NTOK], F32)
            Sf = sb("Sf", [128, 256], F32)
            Sb = sb("Sbf", [128, 256], BF16)
            r_qkv, r_SP, r_BC, r_EX, r_qin, r_kin, r_kout, r_DEC = (Res(n) for n in
                                                                    ("qkv", "SP", "BC", "EX", "qin", "kin", "kout", "DEC"))
            rVTOK = [Res("VTOK%d" % c) for c in range(NCH)]
            rOF = [Res("OF%d" % c) for c in range(NCH)]
            r_Sf, r_Sb = Res("Sf"), Res("Sb")
            d_qkv = S.dsem()
            bankG = pst("bankG", [128, 512], F32)
            r_bankG = Res("bankG")
            psgrot = Rot([(bankG, r_bankG)])
            psn, r_psn = bankG, r_bankG
            attrot = Rot([(pst("att", [128, 512], F32)[:, 0:128], Res("pa")) for _ in range(2)])
            psorot = Rot([(pst("pso", [128, 512], F32)[:, 0:256], Res("po")) for _ in range(2)])
            pkvrot = Rot([(pst("pkv", [128, 512], F32)[:, 0:256], Res("pkv")) for _ in range(1)])
            bkTs = [(pst("bkT", [128, 1024], BF16), Res("bkT")) for _ in range(2)]
            pvtrot = Rot([(bkTs[i][0][:, 0:256], bkTs[i][1]) for i in range(2)])
            pktrot = Rot([(bkTs[i][0][:, 0:512], bkTs[i][1]) for i in range(2)])
            koTall = sb("koTall", [128, NCS, 128], BF16)
            r_koT = Res("koTall")
            attmrot = Rot([(sb("attm", [128, 128], BF16), Res("attm")) for _ in range(2)])
            sqrot = Rot([(sb("sqg", [128, LT], BF16), Res("sqg")) for _ in range(2)])
            ri = sb("rig", [128, LT], F32)
            r_ri = Res("ri")
            OG = sb("OG", [128, 2, LT], BF16)
            r_OG = Res("OG")
            d_og = S.dsem()
            sgo = sb("sgo", [128, 2, LT], F32)
            r_sgo = Res("sgo")
            t1rot = Rot([(sb("t1", [128, LT], F32), Res("t1")) for _ in range(2)])
            stgrot = Rot([(sb("cstg", [128, 2, LT], BF16), Res("cstg"), S.dsem()) for _ in range(2)])

            for h in range(4):
                S.dma("sp", [(qT[:], P[h * 128:(h + 1) * 128, :]), (kT[:], P[512 + h * 128:512 + (h + 1) * 128, :]),
                             (vT[:, 0, :], P[1024 + h * 256:1024 + h * 256 + 128, :]),
                             (vT[:, 1, :], P[1024 + h * 256 + 128:1024 + h * 256 + 256, :])], [], [r_qkv], d_qkv)
                for c in range(NCH):
                    pa, pr = pvtrot.next()

                    def fn(e, pa=pa, c=c):
                        ins = None
                        for vc in range(2):
                            ins = e.transpose(pa[:, vc * 128:(vc + 1) * 128], vT[:, vc, c * 128:(c + 1) * 128], self.ident_bf[:])
                        return ins
                    S.op("pe", fn, [r_qkv], [pr])
                    S.op("act", lambda e, pa=pa, c=c: e.copy(out=VTOK[:, c, :], in_=pa), [pr], [rVTOK[c]])
                for d in range(2):
                    lr = lrf if d == 0 else lrb
                    RM = RMF if d == 0 else RMB
                    segs = list(range(NSEG)) if d == 0 else list(range(NSEG - 1, -1, -1))
                    first = True
                    for si, s in enumerate(segs):
                        s0 = s * SEG
                        for t in range(SEG // LT):
                            sl = slice(t * LT, (t + 1) * LT)
                            gsl = slice(s0 + t * LT, s0 + (t + 1) * LT)
                            pga, pgr = psgrot.next()
                            S.op("pe", lambda e, pga=pga, d=d, h=h, lr=lr, gsl=gsl: e.matmul(
                                pga[:, 0:LT], lhsT=gub[:, d, h * 128:(h + 1) * 128], rhs=lr[:, gsl], start=True, stop=True),
                                [r_par, r_lr], [pgr])
                            S.op("act", lambda e, pga=pga, d=d, h=h, sl=sl: e.activation(
                                out=EX[:, sl], in_=pga[:, 0:LT], func=AF.Exp, scale=-1.0, bias=negb[:, d * 4 + h:d * 4 + h + 1]),
                                [pgr, r_par], [r_EX])
                            S.op("act", lambda e, sl=sl: e.activation(out=SP[:, sl], in_=EX[:, sl], func=AF.Ln,
                                                                      bias=self.one_col[:]), [r_EX], [r_SP])
                        if d == 0:
                            S.op("dve", lambda e, RM=RM: e.tensor_tensor_scan(out=BC[:], data0=RM[:], data1=SP[:], initial=0.0,
                                                                              op0=ALU.mult, op1=ALU.add), [r_SP, r_par], [r_BC])
                        else:
                            S.op("dve", lambda e, RM=RM: e.tensor_tensor_scan(out=BC[:, ::-1], data0=RM[:, ::-1], data1=SP[:, ::-1],
                                                                              initial=0.0, op0=ALU.mult, op1=ALU.add),
                                 [r_SP, r_par], [r_BC])
                        S.op("act", lambda e: e.activation(out=EX[:], in_=BC[:], func=AF.Exp, scale=-1.0 / 16), [r_BC], [r_EX])
                        S.op("dve", lambda e, s0=s0: e.scalar_tensor_tensor(out=qin[:], in0=qT[:, s0:s0 + SEG], scalar=128.0 ** -0.5,
                                                                          in1=EX[:], op0=ALU.mult, op1=ALU.mult),
                             [r_qkv, r_EX], [r_qin])
                        col = 127 if d == 0 else 0
                        S.op("dve", lambda e, col=col: e.tensor_copy(
                            out=DEC[:], in_=EX[:].rearrange("p (c j) -> p c j", j=128)[:, :, col]), [r_EX], [r_DEC])
                        S.op("act", lambda e: e.activation(out=SP[:], in_=BC[:], func=AF.Exp, scale=1.0 / 16), [r_BC], [r_SP])
                        S.op("dve", lambda e, s0=s0: e.tensor_tensor(out=kin[:], in0=kT[:, s0:s0 + SEG], in1=SP[:], op=ALU.mult),
                             [r_qkv, r_SP], [r_kin])
                        for cl in range(NCS):
                            S.op("dve", lambda e, cl=cl: e.tensor_scalar(
                                out=kout[:, cl * 128:(cl + 1) * 128], in0=kin[:, cl * 128:(cl + 1) * 128],
                                scalar1=DEC[:, cl:cl + 1], scalar2=None, op0=ALU.mult), [r_kin, r_DEC], [r_kout])
                        for g0 in range(0, NCS, 4):
                            gn = min(4, NCS - g0)
                            kta, ktr = pktrot.next()

                            def fnt(e, kta=kta, g0=g0, gn=gn):
                                ins = None
                                for i in range(gn):
                                    ins = e.transpose(kta[:, i * 128:(i + 1) * 128], kout[:, (g0 + i) * 128:(g0 + i + 1) * 128],
                                                      self.ident_bf[:])
                                return ins
                            S.op("pe", fnt, [r_kout], [ktr])
                            S.op("act", lambda e, kta=kta, g0=g0, gn=gn: e.copy(
                                out=koTall[:, g0:g0 + gn, :], in_=kta[:, 0:gn * 128].rearrange("p (c j) -> p c j", j=128)),
                                [ktr], [r_koT])
                        if si > 0:
                            S.op("dve", lambda e: e.tensor_scalar(out=Sf[:], in0=Sf[:], scalar1=self.carry[:, 0:1], scalar2=None,
                                                                  op0=ALU.mult), [r_Sf], [r_Sf])
                            S.op("act", lambda e: e.copy(out=Sb[:], in_=Sf[:]), [r_Sf], [r_Sb])
                        cls = list(range(NCS)) if d == 0 else list(range(NCS - 1, -1, -1))
                        for ci, cl in enumerate(cls):
                            c = s * NCS + cl
                            ls = slice(cl * 128, (cl + 1) * 128)
                            last = (si == NSEG - 1 and ci == NCS - 1)
                            aa, ar = attrot.next()
                            S.op("pe", lambda e, aa=aa, ls=ls: e.matmul(aa, lhsT=kin[:, ls], rhs=qin[:, ls], start=True, stop=True),
                                 [r_kin, r_qin], [ar])
                            if not last:
                                pka, pkr = pkvrot.next()
                                S.op("pe", lambda e, pka=pka, cl=cl, c=c: e.matmul(pka, lhsT=koTall[:, cl, :], rhs=VTOK[:, c, :],
                                                                                   start=True, stop=True), [r_koT, rVTOK[c]], [pkr])
                            ma, mr = attmrot.next()
                            S.op("dve", lambda e, ma=ma, aa=aa, d=d: e.tensor_tensor(out=ma[:], in0=aa, in1=gmask[:, d, :],
                                                                                     op=ALU.mult), [ar, r_par], [mr])
                            poa, por = psorot.next()

                            def fno(e, poa=poa, ma=ma, ls=ls, c=c, first=first):
                                ins = None
                                for vc in range(2):
                                    vs = slice(vc * 128, (vc + 1) * 128)
                                    if not first:
                                        e.matmul(poa[:, vs], lhsT=Sb[:, vs], rhs=qin[:, ls], start=True, stop=False)
                                    ins = e.matmul(poa[:, vs], lhsT=VTOK[:, c, vs], rhs=ma[:], start=first, stop=True)
                                return ins
                            S.op("pe", fno, [mr, rVTOK[c], r_qin] + ([] if first else [r_Sb]), [por])
                            gs = slice(c * 128, (c + 1) * 128)
                            pview = poa.rearrange("p (v t) -> p v t", t=128)
                            if d == 0:
                                S.op("act", lambda e, pview=pview, gs=gs: e.copy(out=OF[:, :, gs], in_=pview), [por], [rOF[c]])
                            else:
                                S.op("dve", lambda e, pview=pview, gs=gs: e.tensor_tensor(out=OF[:, :, gs], in0=pview, in1=OF[:, :, gs],
                                                                                          op=ALU.add), [por, rOF[c]], [rOF[c]])
                            if not last:
                                if first:
                                    S.op("dve", lambda e, pka=pka: e.tensor_copy(out=Sf[:], in_=pka), [pkr], [r_Sf])
                                else:
                                    S.op("dve", lambda e, pka=pka, cl=cl: e.scalar_tensor_tensor(
                                        out=Sf[:], in0=Sf[:], scalar=DEC[:, cl:cl + 1], in1=pka, op0=ALU.mult, op1=ALU.add),
                                        [pkr, r_Sf, r_DEC], [r_Sf])
                                S.op("act", lambda e: e.copy(out=Sb[:], in_=Sf[:]), [r_Sf], [r_Sb])
                            first = False
                for t in range(NTOK // LT):
                    sl = slice(t * LT, (t + 1) * LT)
                    rof = rOF[t * LT // 128:(t + 1) * LT // 128]
                    S.dma("sp", [(OG[:, vc, :], P[2048 + h * 256 + vc * 128:2048 + h * 256 + (vc + 1) * 128, sl]) for vc in range(2)],
                          [], [r_OG], d_og)
                    for vc in range(2):
                        qa, qr = sqrot.next()
                        S.op("act", lambda e, qa=qa, vc=vc, sl=sl: e.activation(out=qa[:], in_=OF[:, vc, sl], func=AF.Square),
                             rof, [qr])
                        S.op("pe", lambda e, qa=qa, vc=vc: e.matmul(psn[:, 0:LT], lhsT=self.ones_bf[:], rhs=qa[:],
                                                                   start=(vc == 0), stop=(vc == 1)), [qr], [r_psn])
                    S.op("act", lambda e: e.activation(out=ri[:], in_=psn[:, 0:LT], func=AF.Sqrt, scale=1.0 / 256,
                                                       bias=self.eps_col[:]), [r_psn], [r_ri])
                    S.op("dve", lambda e: e.reciprocal(out=ri[:], in_=ri[:]), [r_ri], [r_ri])
                    S.op("act", lambda e: e.activation(out=sgo[:], in_=OG[:], func=AF.Silu), [r_OG], [r_sgo])
                    sta, str_, std = stgrot.next()
                    for vc in range(2):
                        ta, tr = t1rot.next()
                        S.op("dve", lambda e, ta=ta, vc=vc, sl=sl: e.scalar_tensor_tensor(
                            out=ta[:], in0=OF[:, vc, sl], scalar=odp[:, 8 + vc:9 + vc], in1=ri[:], op0=ALU.mult, op1=ALU.mult),
                            rof + [r_ri, r_par], [tr])
                        S.op("dve", lambda e, ta=ta, sta=sta, vc=vc: e.tensor_tensor(out=sta[:, vc, :], in0=ta[:], in1=sgo[:, vc, :],
                                                                                     op=ALU.mult), [tr, r_sgo], [str_])
                    S.dma("sp", [(Mo[h * 256:(h + 1) * 256, sl].rearrange("(v p) n -> p v n", p=128), sta[:])], [str_], [], std)
            S.barrier()
            S.flush()
            S.release(d0, d_qkv, d_og, *prel, *[x[2] for x in stgrot.items])

    def odd_lru(self, k):
        nc, S, I = self.nc, self.S, self.I
        l = k // 2
        NTOK, SEG = self.NTOK, self.SEG
        NSEG = NTOK // SEG
        LT = 512
        NT = NTOK // LT
        P, Mo = self.P, self.M
        with ExitStack() as es:
            def sb(name, shape, dt):
                return es.enter_context(nc.sbuf_tensor(uname(name), list(shape), dt))

            def pst(name, shape, dt):
                return es.enter_context(nc.psum_tensor(uname(name), list(shape), dt))

            odp = sb("odp", [128, NOD], F32)
            clam = sb("clam", [128, 16], F32)
            wab = sb("wab", [128, 2, 8, 128], BF16)
            wxb = sb("wxb", [128, 2, 8, 128], BF16)
            r_par = Res("par")
            d0 = S.dsem()
            d0s = S.dsem(True)
            S.dma("pool", [(wab[:], I["dwa"][l].rearrange("d c i j -> i d c j")),
                           (wxb[:], I["dwx"][l].rearrange("d c i j -> i d c j"))], [], [r_par], d0s)
            prel = []
            S.dma("sp", [(odp[:], I["odp"][:, l, :])], [], [r_par], d0)
            S.op("act", lambda e: e.activation(out=clam[:], in_=odp[:, 82:98], func=AF.Exp, scale=-1.0), [r_par], [r_par])
            S.op("act", lambda e: e.activation(out=clam[:], in_=clam[:], func=AF.Ln, bias=self.one_col[:]), [r_par], [r_par])
            S.op("dve", lambda e: e.tensor_scalar(out=clam[:], in0=clam[:], scalar1=-8.0, scalar2=None, op0=ALU.mult),
                 [r_par], [r_par])
            XS = sb("XS", [128, NSEG, SEG + 3], BF16)
            XC = sb("XC", [128, NTOK], F32)
            XCB = sb("XCB", [128, NTOK], BF16)
            YG = sb("YG", [128, NTOK], BF16)
            A = [sb("A%d" % d, [128, NTOK], F32) for d in range(2)]
            U = [sb("U%d" % d, [128, NTOK], F32) for d in range(2)]
            Hh = [sb("H%d" % d, [128, NTOK], F32) for d in range(2)]
            r_XS, r_XC, r_XCB, r_YG = Res("XS"), Res("XC"), Res("XCB"), Res("YG")
            r_A = [Res("A0"), Res("A1")]
            r_U = [Res("U0"), Res("U1")]
            r_H = [Res("H0"), Res("H1")]
            d_x, d_y = S.dsem(), S.dsem()
            psrot = Rot([(pst("psl", [128, 512], F32), Res("psl")) for _ in range(4)])
            Rrot = Rot([(sb("R", [128, LT], F32), Res("R")) for _ in range(2)])
            Irot = Rot([(sb("Ig", [128, LT], F32), Res("Ig")) for _ in range(2)])
            A2rot = Rot([(sb("A2", [128, LT], F32), Res("A2")) for _ in range(2)])
            y2rot = Rot([(sb("y2", [128, LT], F32), Res("y2")) for _ in range(4)])
            sgrot = Rot([(sb("sgl", [128, LT], F32), Res("sgl")) for _ in range(2)])
            stgrot = Rot([(sb("dstg", [128, LT], BF16), Res("dstg"), S.dsem()) for _ in range(2)])
            GLb = [sb("GL", [128, NTOK], BF16) for _ in range(2)]
            r_GL = [Res("GL0"), Res("GL1")]
            hsrot = Rot([(sb("hs", [128, LT], F32), Res("hs")) for _ in range(2)])

            def head(c):
                row = 3104 + c * 128
                S.dma("sp", [(XS[:, s, 1:1 + SEG], P[row:row + 128, s * SEG:(s + 1) * SEG]) for s in range(NSEG)], [], [r_XS], d_x)
                S.dma("sp", [(YG[:], P[4128 + c * 128:4128 + (c + 1) * 128, :])], [], [r_YG], d_y)
                for s in range(NSEG):
                    if s == 0:
                        S.op("dve", lambda e, s=s: e.memset(XS[:, s, 0:1], 0.0), [], [r_XS])
                    else:
                        S.op("dve", lambda e, s=s: e.tensor_scalar(out=XS[:, s, 0:1], in0=XS[:, s - 1, SEG:SEG + 1],
                                                                  scalar1=self.carry[:, 0:1], scalar2=None, op0=ALU.mult),
                             [r_XS], [r_XS])
                    if s == NSEG - 1:
                        S.op("dve", lambda e, s=s: e.memset(XS[:, s, SEG + 1:SEG + 3], 0.0), [], [r_XS])
                    else:
                        S.op("dve", lambda e, s=s: e.tensor_scalar(out=XS[:, s, SEG + 1:SEG + 3], in0=XS[:, s + 1, 1:3],
                                                                  scalar1=self.carry[:, 0:1], scalar2=None, op0=ALU.mult),
                             [r_XS], [r_XS])
                wc = 10 + c * 4
                for s in range(NSEG):
                    xs = slice(s * SEG, (s + 1) * SEG)
                    S.op("dve", lambda e, s=s, xs=xs: e.tensor_scalar(
                        out=XC[:, xs], in0=XS[:, s, 0:SEG], scalar1=odp[:, wc:wc + 1], scalar2=odp[:, 42 + c:43 + c],
                        op0=ALU.mult, op1=ALU.add), [r_XS, r_par], [r_XC])
                    for j in range(1, 4):
                        S.op("dve", lambda e, s=s, xs=xs, j=j: e.scalar_tensor_tensor(
                            out=XC[:, xs], in0=XS[:, s, j:j + SEG], scalar=odp[:, wc + j:wc + j + 1], in1=XC[:, xs],
                            op0=ALU.mult, op1=ALU.add), [r_XS, r_XC, r_par], [r_XC])
                S.op("act", lambda e: e.copy(out=XCB[:], in_=XC[:]), [r_XC], [r_XCB])
                GL, rg = GLb[c % 2], r_GL[c % 2]
                for t in range(NT):
                    sl = slice(t * LT, (t + 1) * LT)
                    ya, yr = y2rot.next()
                    sa, sr = sgrot.next()
                    S.op("pool", lambda e, ya=ya, sl=sl: e.tensor_tensor(out=ya[:], in0=YG[:, sl], in1=YG[:, sl], op=ALU.mult),
                         [r_YG], [yr])
                    S.op("pool", lambda e, ya=ya: e.tensor_scalar(out=ya[:], in0=ya[:], scalar1=0.044715, scalar2=1.0,
                                                                  op0=ALU.mult, op1=ALU.add), [yr], [yr])
                    S.op("pool", lambda e, ya=ya, sl=sl: e.tensor_tensor(out=ya[:], in0=ya[:], in1=YG[:, sl], op=ALU.mult),
                         [yr, r_YG], [yr])
                    S.op("act", lambda e, ya=ya, sa=sa: e.activation(out=sa[:], in_=ya[:], func=AF.Sigmoid, scale=1.5957691216057308),
                         [yr], [sr])
                    S.op("dve", lambda e, sa=sa, sl=sl, GL=GL: e.tensor_tensor(out=GL[:, sl], in0=sa[:], in1=YG[:, sl], op=ALU.mult),
                         [sr, r_YG], [rg])

            def gates(c):
                for d in range(2):
                    pc = d * 8 + c
                    for t in range(NT):
                        sl = slice(t * LT, (t + 1) * LT)
                        pra, prr = psrot.next()
                        pia, pir = psrot.next()
                        S.op("pe", lambda e, pra=pra, d=d, sl=sl: e.matmul(pra[:], lhsT=wab[:, d, c, :], rhs=XCB[:, sl],
                                                                          start=True, stop=True), [r_XCB, r_par], [prr])
                        S.op("pe", lambda e, pia=pia, d=d, sl=sl: e.matmul(pia[:], lhsT=wxb[:, d, c, :], rhs=XCB[:, sl],
                                                                          start=True, stop=True), [r_XCB, r_par], [pir])
                        Ra, Rr = Rrot.next()
                        Ia, Ir = Irot.next()
                        A2a, A2r = A2rot.next()
                        S.op("act", lambda e, Ra=Ra, pra=pra, pc=pc: e.activation(out=Ra[:], in_=pra[:], func=AF.Sigmoid,
                                                                                bias=odp[:, 50 + pc:51 + pc]), [prr, r_par], [Rr])
                        S.op("act", lambda e, Ia=Ia, pia=pia, pc=pc: e.activation(out=Ia[:], in_=pia[:], func=AF.Sigmoid,
                                                                                bias=odp[:, 66 + pc:67 + pc]), [pir, r_par], [Ir])
                        S.op("act", lambda e, Ra=Ra, d=d, sl=sl, pc=pc: e.activation(out=A[d][:, sl], in_=Ra[:], func=AF.Exp,
                                                                                   scale=clam[:, pc:pc + 1]), [Rr, r_par], [r_A[d]])
                        S.op("dve", lambda e, A2a=A2a, d=d, sl=sl: e.tensor_tensor(out=A2a[:], in0=A[d][:, sl], in1=A[d][:, sl],
                                                                                 op=ALU.mult), [r_A[d]], [A2r])
                        S.op("act", lambda e, A2a=A2a: e.activation(out=A2a[:], in_=A2a[:], func=AF.Sqrt, scale=-1.0,
                                                                    bias=self.one_col[:]), [A2r], [A2r])
                        S.op("dve", lambda e, Ia=Ia, sl=sl: e.tensor_tensor(out=Ia[:], in0=Ia[:], in1=XC[:, sl], op=ALU.mult),
                             [Ir, r_XC], [Ir])
                        S.op("dve", lambda e, Ia=Ia, A2a=A2a, d=d, sl=sl: e.tensor_tensor(out=U[d][:, sl], in0=Ia[:], in1=A2a[:],
                                                                                        op=ALU.mult), [Ir, A2r], [r_U[d]])
                    for s in range(1, NSEG):
                        tb = s * SEG if d == 0 else s * SEG - 1
                        S.op("dve", lambda e, d=d, tb=tb: e.tensor_scalar(out=A[d][:, tb:tb + 1], in0=A[d][:, tb:tb + 1],
                                                                         scalar1=self.carry[:, 0:1], scalar2=None, op0=ALU.mult),
                             [r_A[d]], [r_A[d]])

            def scans(c):
                S.op("dve", lambda e: e.tensor_tensor_scan(out=Hh[0][:], data0=A[0][:], data1=U[0][:], initial=0.0,
                                                           op0=ALU.mult, op1=ALU.add), [r_A[0], r_U[0]], [r_H[0]])
                S.op("dve", lambda e: e.tensor_tensor_scan(out=Hh[1][:, ::-1], data0=A[1][:, ::-1], data1=U[1][:, ::-1],
                                                           initial=0.0, op0=ALU.mult, op1=ALU.add),
                     [r_A[1], r_U[1]], [r_H[1]])

            def tail(c):
                GL, rg = GLb[c % 2], r_GL[c % 2]
                for t in range(NT):
                    sl = slice(t * LT, (t + 1) * LT)
                    ha, hr = hsrot.next()
                    S.op("dve", lambda e, ha=ha, sl=sl: e.tensor_tensor(out=ha[:], in0=Hh[0][:, sl], in1=Hh[1][:, sl], op=ALU.add),
                         [r_H[0], r_H[1]], [hr])
                    sta, str_, std = stgrot.next()
                    S.op("dve", lambda e, ha=ha, sta=sta, sl=sl, GL=GL: e.tensor_tensor(out=sta[:], in0=ha[:], in1=GL[:, sl], op=ALU.mult),
                         [hr, rg], [str_])
                    S.dma("sp", [(Mo[1024 + c * 128:1024 + (c + 1) * 128, sl], sta[:])], [str_], [], std)

            head(0)
            for c in range(8):
                gates(c)
                if c + 1 < 8:
                    head(c + 1)
                scans(c)
                tail(c)
            S.barrier()
            S.flush()
            S.release(d0, d0s, d_x, d_y, *prel, *[x[2] for x in stgrot.items])


NEV = 282
NOD = 128


def host_consts():
    out = {}
    out["ident"] = np.eye(128, dtype=np.float32)
    sp = np.arange(128)[:, None]
    tq = np.arange(128)[None, :]
    ab = np.zeros((128, 8, 384), np.float32)
    for h in range(8):
        slope = 2.0 ** (-(h + 1))
        for j in range(3):
            rel = (j - 1) * 128 + sp - tq
            ab[:, h, j * 128:(j + 1) * 128] = np.where(np.abs(rel) <= 128, -slope * np.abs(rel), -30000.0)
    out["abias"] = ab
    gm = np.zeros((128, 2, 128), np.float32)
    gm[:, 0, :] = (sp <= tq)
    gm[:, 1, :] = (sp >= tq)
    out["gmask"] = gm
    return out


def host_params(inputs, depth):
    n_even = (depth + 1) // 2
    n_odd = depth // 2
    f = lambda nm: np.asarray(inputs[nm], np.float32)
    out = {}
    lnp = np.zeros((128, 3, depth, KCD), np.float32)
    for i, nm in enumerate(("ln_ffn1", "ln_mix", "ln_ffn2")):
        lnp[:, i] = f(nm)[:depth].reshape(depth, KCD, 128).transpose(2, 0, 1)
    out["lnp"] = lnp
    evp = np.zeros((128, n_even, NEV), np.float32)
    for l in range(n_even):
        evp[:, l, 0] = f("a_q_gain")[l]
        evp[:, l, 1] = f("a_k_gain")[l]
        evp[:, l, 2:10] = f("a_sink")[l][None, :]
        cw = f("b_conv_w")[l]
        evp[:, l, 10:258] = cw.reshape(31, 8, 128).transpose(2, 1, 0).reshape(128, 248)
        evp[:, l, 258:266] = f("b_conv_b")[l].reshape(8, 128).T
        evp[:, l, 266:274] = f("b_norm_g")[l].reshape(8, 128).T
        evp[:, l, 274:282] = f("b_norm_b")[l].reshape(8, 128).T
    out["evp"] = evp
    if n_odd:
        odp = np.zeros((128, n_odd, NOD), np.float32)
        for l in range(n_odd):
            odp[:, l, 0:8] = f("c_gate_bias")[l].reshape(2, 4, 128).transpose(2, 0, 1).reshape(128, 8)
            odp[:, l, 8:10] = f("c_norm_g")[l].reshape(2, 128).T
            odp[:, l, 10:42] = f("d_conv_w")[l].reshape(4, 8, 128).transpose(2, 1, 0).reshape(128, 32)
            odp[:, l, 42:50] = f("d_conv_b")[l].reshape(8, 128).T
            odp[:, l, 50:66] = f("d_ba")[l].reshape(2, 8, 128).transpose(2, 0, 1).reshape(128, 16)
            odp[:, l, 66:82] = f("d_bx")[l].reshape(2, 8, 128).transpose(2, 0, 1).reshape(128, 16)
            odp[:, l, 82:98] = f("d_lambda")[l].reshape(2, 8, 128).transpose(2, 0, 1).reshape(128, 16)
        out["odp"] = odp
        out["gup"] = f("c_gate_up")[:n_odd]
        out["dwa"] = f("d_wa")[:n_odd]
        out["dwx"] = f("d_wx")[:n_odd]
    for nm in ("ffn1_w_in", "ffn1_w_out", "ffn2_w_in", "ffn2_w_out"):
        out[nm] = f(nm)[:depth]
    out["ev_w_in"] = f("ev_w_in")[:n_even]
    out["ev_w_out"] = f("ev_w_out")[:n_even]
    if n_odd:
        out["od_w_in"] = f("od_w_in")[:n_odd]
        out["od_w_out"] = f("od_w_out")[:n_odd]
    return out


def build_program(cfg):
    b = Builder(cfg)
    nc = b.build()
    return nc, b


_CACHE = {}


def kernel(**inputs):
    depth = 4
    cfg = {"NTOK": 4096, "SEG": 2048, "DEPTH": depth}
    nc, b = build_program(cfg)
    shared = host_params(inputs, depth)
    shared.update(host_consts())
    xp = np.asarray(inputs["x_prompt"], np.float32)
    xs = np.asarray(inputs["x_sample"], np.float32)
    in_maps = []
    for c in range(8):
        m = dict(shared)
        if c < 4:
            m["xT"] = np.ascontiguousarray(xp[2 * c:2 * c + 2].reshape(4096, D_MODEL).T)
            m["carry"] = np.zeros((128, 1), np.float32)
        else:
            m["xT"] = np.ascontiguousarray(xs[c - 4].T)
            m["carry"] = np.ones((128, 1), np.float32)
        in_maps.append(m)
    res = run_bass_kernel_spmd(nc, in_maps, core_ids=list(range(8)))
    yp = np.empty_like(xp)
    ys = np.empty_like(xs)
    for c in range(8):
        y = res.results[c]["yT"].T
        if c < 4:
            yp[2 * c:2 * c + 2] = y.reshape(2, 2048, D_MODEL)
        else:
            ys[c - 4] = y
    return (yp, ys)
```
